# Optimizing a Trainium2 kernel written in Bass

```python
import math
import jax, jax.numpy as jnp
from jax import lax
import numpy as np

D_MODEL = 2048
BATCH = 16
SEQ = 256
DEPTH = 2
DEC_BATCH = 2
DEC_SEQ = 4096
PAST_LEN = 256

GRID_W = 64
D_MIX = 2 * D_MODEL
SSD_WIDTH = D_MIX // 2
SSD_HEAD_DIM = 64
SSD_HEADS = SSD_WIDTH // SSD_HEAD_DIM
SSD_GROUPS = 4
SSD_STATE = 128
SSD_CHUNK = 128
CONV_K = 5
CONV_DIM = SSD_WIDTH + 2 * SSD_GROUPS * SSD_STATE
FNET_WIDTH = D_MIX // 4
FNET_GROUPS = 4
SGU_WIDTH = D_MIX // 4
SGU_HEADS = 8
SGU_HEAD_DIM = SGU_WIDTH // SGU_HEADS
SGU_CHUNK = 128
D_FF = 5632
N_MOD = 9
EPS = 1e-6
OFF_XBC = SSD_WIDTH
OFF_DT = OFF_XBC + CONV_DIM
OFF_FNET = OFF_DT + 2 * SSD_HEADS
OFF_SGU = OFF_FNET + FNET_WIDTH
IN_COLS = OFF_SGU + 2 * SGU_WIDTH

kernel_name = "hybrid_ssd_fnet_sgu_diffusion_step"


def rms_norm(x, g):
    xf = x.astype(jnp.float32)
    xf = xf * lax.rsqrt(jnp.mean(xf * xf, axis=-1, keepdims=True) + EPS)
    return (xf * g.astype(jnp.float32)).astype(x.dtype)


def swiglu(h, w_up, w_down):
    g, u = jnp.split(h @ w_up, 2, axis=-1)
    return (jax.nn.silu(g) * u) @ w_down


def depthwise_conv(u, w, b, grid_rows):
    bsz, L, C = u.shape
    if grid_rows:
        rows = L // GRID_W
        u = u.reshape(bsz * rows, GRID_W, C)
    out = lax.conv_general_dilated(
        u, w[:, None, :].astype(u.dtype), window_strides=(1,),
        padding=[(CONV_K // 2, CONV_K // 2)],
        dimension_numbers=('NWC', 'WIO', 'NWC'), feature_group_count=C)
    return out.reshape(bsz, L, C) + b


def segsum_exp(a_cs):
    T = a_cs.shape[-1]
    diff = a_cs[..., :, None] - a_cs[..., None, :]
    mask = jnp.tril(jnp.ones((T, T), dtype=bool))
    return jnp.exp(jnp.where(mask, diff, -jnp.inf))


def ssd_scan(xs, dt, a_neg, bmat, cmat, h0):
    b, l, H, P = xs.shape
    G, N = bmat.shape[2], bmat.shape[3]
    R = H // G
    T = SSD_CHUNK
    nc = l // T
    x = (xs * dt[..., None]).reshape(b, nc, T, G, R, P)
    a = jnp.moveaxis((dt * a_neg).reshape(b, nc, T, G, R), 2, -1)
    a_cs = jnp.cumsum(a, axis=-1)
    Bc = bmat.reshape(b, nc, T, G, N)
    Cc = cmat.reshape(b, nc, T, G, N)
    Lm = segsum_exp(a_cs)
    cb = jnp.einsum('bctgn,bcsgn->bcgts', Cc, Bc)
    y_diag = jnp.einsum('bcgts,bcgrts,bcsgrp->bctgrp', cb, Lm, x)
    decay_to_end = jnp.exp(a_cs[..., -1:] - a_cs)
    states = jnp.einsum('bctgn,bcgrt,bctgrp->bcgrpn', Bc, decay_to_end, x)
    chunk_decay = jnp.exp(a_cs[..., -1])

    def step(h, inp):
        s, d = inp
        return h * d[..., None, None] + s, h

    h_final, h_prev = lax.scan(step, h0.reshape(b, G, R, P, N),
                               (jnp.moveaxis(states, 1, 0), jnp.moveaxis(chunk_decay, 1, 0)))
    h_prev = jnp.moveaxis(h_prev, 0, 1)
    y_off = jnp.einsum('bctgn,bcgrpn,bcgrt->bctgrp', Cc, h_prev, jnp.exp(a_cs))
    y = (y_diag + y_off).reshape(b, l, H, P)
    return y, h_final.reshape(b, H, P, N)


def token_mixer(h, w_in, conv_w, conv_b, a_log, dt_bias, d_skip, g_ssd, g_sgu, w_sp, b_sp, w_out,
                h0, grid_rows):
    bsz, L, _ = h.shape
    f32 = jnp.float32
    proj = h @ w_in
    z = proj[..., :SSD_WIDTH]
    xbc = proj[..., OFF_XBC:OFF_DT]
    dt_raw = proj[..., OFF_DT:OFF_FNET]
    f_in = proj[..., OFF_FNET:OFF_SGU]
    s_in = proj[..., OFF_SGU:]

    xbc = jax.nn.silu(depthwise_conv(xbc, conv_w, conv_b, grid_rows))
    gn = SSD_GROUPS * SSD_STATE
    xs = xbc[..., :SSD_WIDTH].reshape(bsz, L, SSD_HEADS, SSD_HEAD_DIM).astype(f32)
    bm = xbc[..., SSD_WIDTH:SSD_WIDTH + gn].reshape(bsz, L, SSD_GROUPS, SSD_STATE).astype(f32)
    cm = xbc[..., SSD_WIDTH + gn:].reshape(bsz, L, SSD_GROUPS, SSD_STATE).astype(f32)
    dt = jax.nn.softplus(dt_raw.astype(f32).reshape(bsz, L, 2, SSD_HEADS) + dt_bias.astype(f32))
    a_neg = -jnp.exp(a_log.astype(f32))
    h0f = h0.astype(f32)
    y_f, hf_f = ssd_scan(xs, dt[:, :, 0], a_neg[0], bm, cm, h0f[:, 0])
    y_b, hf_b = ssd_scan(xs[:, ::-1], dt[:, ::-1, 1], a_neg[1], bm[:, ::-1], cm[:, ::-1], h0f[:, 1])
    d_sum = (d_skip[0] + d_skip[1]).astype(f32)
    y = y_f + y_b[:, ::-1] + xs * d_sum[:, None]
    y = y.reshape(bsz, L, SSD_WIDTH).astype(h.dtype)
    y_ssd = rms_norm(y * jax.nn.silu(z), g_ssd)
    new_h = jnp.stack([hf_f, hf_b], axis=1).astype(h.dtype)

    f = f_in.reshape(bsz, L, FNET_GROUPS, FNET_WIDTH // FNET_GROUPS).astype(f32)
    y_fnet = jnp.real(jnp.fft.fft2(f, axes=(1, 3), norm='ortho'))
    y_fnet = y_fnet.reshape(bsz, L, FNET_WIDTH).astype(h.dtype)

    u, v = jnp.split(jax.nn.gelu(s_in), 2, axis=-1)
    v = rms_norm(v, g_sgu).reshape(bsz, L // SGU_CHUNK, SGU_CHUNK, SGU_HEADS, SGU_HEAD_DIM)
    sp = jnp.einsum('hts,bkshd->bkthd', w_sp, v) + b_sp.T[None, None, :, :, None]
    y_sgu = u * sp.reshape(bsz, L, SGU_WIDTH)

    out = jnp.concatenate([y_ssd, y_fnet, y_sgu], axis=-1) @ w_out
    return out, new_h


def trunk_layer(x, mod, h0, grid_rows, norm_g, w_ffn1_up, w_ffn1_down, w_in, conv_w, conv_b,
                a_log, dt_bias, d_skip, g_ssd, g_sgu, w_sp, b_sp, w_out, w_ffn2_up, w_ffn2_down):
    sh1, sc1, g1, sh2, sc2, g2, sh3, sc3, g3 = jnp.split(mod, N_MOD, axis=-1)
    h = rms_norm(x, norm_g[0]) * (1 + sc1) + sh1
    x = x + 0.5 * g1 * rms_norm(swiglu(h, w_ffn1_up, w_ffn1_down), norm_g[1])
    h = rms_norm(x, norm_g[2]) * (1 + sc2) + sh2
    m, new_h = token_mixer(h, w_in, conv_w, conv_b, a_log, dt_bias, d_skip, g_ssd, g_sgu,
                           w_sp, b_sp, w_out, h0, grid_rows)
    x = x + g2 * rms_norm(m, norm_g[3])
    h = rms_norm(x, norm_g[4]) * (1 + sc3) + sh3
    x = x + 0.5 * g3 * rms_norm(swiglu(h, w_ffn2_up, w_ffn2_down), norm_g[5])
    return x, new_h


def setup_inputs(seed: int = 0) -> dict:
    key = jax.random.key(seed)
    ks = jax.random.split(key, 26)
    f32 = jnp.float32

    def nrm(k, shape, s):
        return jax.random.normal(k, shape, f32) * s

    dt0 = jnp.exp(jax.random.uniform(ks[12], (DEPTH, 2, SSD_HEADS), f32,
                                     math.log(1e-3), math.log(1e-1)))
    dt_bias = dt0 + jnp.log(-jnp.expm1(-dt0))
    return {
        "x_prompt": nrm(ks[0], (BATCH, SEQ, D_MODEL), 1.0),
        "x_sample": nrm(ks[1], (DEC_BATCH, DEC_SEQ, D_MODEL), 1.0),
        "state_ssd": nrm(ks[2], (DEC_BATCH, DEPTH, 2, SSD_HEADS, SSD_HEAD_DIM, SSD_STATE), 0.1),
        "c": nrm(ks[3], (DEC_BATCH, D_MODEL), 1.0),
        "c_ctx": nrm(ks[4], (D_MODEL,), 1.0),
        "w_mod": nrm(ks[5], (DEPTH, D_MODEL, N_MOD * D_MODEL), 0.5 * D_MODEL ** -0.5),
        "b_mod": nrm(ks[6], (DEPTH, N_MOD * D_MODEL), 0.01),
        "norm_g": 1.0 + nrm(ks[7], (DEPTH, 6, D_MODEL), 0.02),
        "w_ffn1_up": nrm(ks[8], (DEPTH, D_MODEL, 2 * D_FF), D_MODEL ** -0.5),
        "w_ffn1_down": nrm(ks[9], (DEPTH, D_FF, D_MODEL), D_FF ** -0.5),
        "w_in": nrm(ks[10], (DEPTH, D_MODEL, IN_COLS), D_MODEL ** -0.5),
        "conv_w": nrm(ks[11], (DEPTH, CONV_K, CONV_DIM), CONV_K ** -0.5),
        "conv_b": nrm(ks[13], (DEPTH, CONV_DIM), 0.01),
        "a_log": jnp.log(jax.random.uniform(ks[14], (DEPTH, 2, SSD_HEADS), f32, 1.0, 16.0)),
        "dt_bias": dt_bias,
        "d_skip": 1.0 + nrm(ks[15], (DEPTH, 2, SSD_HEADS), 0.1),
        "g_ssd": 1.0 + nrm(ks[16], (DEPTH, SSD_WIDTH), 0.02),
        "g_sgu": 1.0 + nrm(ks[17], (DEPTH, SGU_WIDTH), 0.02),
        "w_sp": nrm(ks[18], (DEPTH, SGU_HEADS, SGU_CHUNK, SGU_CHUNK), SGU_CHUNK ** -0.5),
        "b_sp": nrm(ks[19], (DEPTH, SGU_HEADS, SGU_CHUNK), 0.01),
        "w_out": nrm(ks[20], (DEPTH, D_MIX, D_MODEL), D_MIX ** -0.5),
        "w_ffn2_up": nrm(ks[21], (DEPTH, D_MODEL, 2 * D_FF), D_MODEL ** -0.5),
        "w_ffn2_down": nrm(ks[22], (DEPTH, D_FF, D_MODEL), D_FF ** -0.5),
    }


def reference(x_prompt, x_sample, state_ssd, c, c_ctx, w_mod, b_mod, norm_g, w_ffn1_up,
              w_ffn1_down, w_in, conv_w, conv_b, a_log, dt_bias, d_skip, g_ssd, g_sgu, w_sp,
              b_sp, w_out, w_ffn2_up, w_ffn2_down):
    y_prompt = x_prompt
    y_sample = x_sample
    h0_ctx = jnp.zeros((x_prompt.shape[0], 2, SSD_HEADS, SSD_HEAD_DIM, SSD_STATE), x_prompt.dtype)
    ctx_states = []
    for l in range(DEPTH):
        layer_params = (norm_g[l], w_ffn1_up[l], w_ffn1_down[l], w_in[l], conv_w[l], conv_b[l],
                        a_log[l], dt_bias[l], d_skip[l], g_ssd[l], g_sgu[l], w_sp[l], b_sp[l],
                        w_out[l], w_ffn2_up[l], w_ffn2_down[l])
        mod_ctx = (jax.nn.silu(c_ctx) @ w_mod[l] + b_mod[l])[None, None, :]
        y_prompt, h_ctx = trunk_layer(y_prompt, mod_ctx, h0_ctx, False, *layer_params)
        ctx_states.append(h_ctx)
        mod_lat = (jax.nn.silu(c) @ w_mod[l] + b_mod[l])[:, None, :]
        y_sample, _ = trunk_layer(y_sample, mod_lat, state_ssd[:, l], True, *layer_params)
    new_state_ssd = jnp.stack(ctx_states, axis=1)
    return (y_prompt, y_sample, new_state_ssd)
```

```python
import contextlib
import numpy as np
import concourse.bass as bass
import concourse.mybir as mybir
from concourse.bass_utils import run_bass_kernel_spmd

F32 = mybir.dt.float32
BF16 = mybir.dt.bfloat16
AF = mybir.ActivationFunctionType
ALU = mybir.AluOpType

D = 2048
DFF = 5632
NPR = 512
NSA = 4096
NTOK = NPR + NSA
NT = NTOK // 512
NCH = NTOK // 128
INC = 8256
EPS = 1e-6
DEPTH = 2
EPOCH = 12000


class T:
    __slots__ = ("w", "rs")

    def __init__(s):
        s.w = None
        s.rs = []


def Ts(n):
    return [T() for _ in range(n)]


class Op:
    __slots__ = ("eng", "fn", "deps", "need", "sv", "dma", "dsem", "dval")

    def __init__(s, eng, fn, dma):
        s.eng = eng; s.fn = fn; s.deps = []; s.need = False; s.sv = 0; s.dma = dma; s.dsem = 0; s.dval = 0


ENGS = ("pe", "act", "dve", "pool", "sp")
DMAQ = ("pool", "sp")


class K:
    def __init__(s, nc):
        s.nc = nc
        s.st = {e: [] for e in ENGS}

    def op(s, eng, fn, r=(), w=(), dma=False):
        o = Op(eng, fn, dma)
        seen = set()
        for t in r:
            d = t.w
            if d is not None and id(d) not in seen:
                seen.add(id(d)); o.deps.append(d)
        for t in w:
            d = t.w
            if d is not None and id(d) not in seen:
                seen.add(id(d)); o.deps.append(d)
            last = {}
            for d in t.rs:
                last[d.eng if not d.dma else id(d)] = d
            for d in last.values():
                if id(d) not in seen:
                    seen.add(id(d)); o.deps.append(d)
        if eng == "pe":
            o.deps = [d for d in o.deps if not (d.eng == "pe" and not d.dma)]
        for d in o.deps:
            d.need = True
        for t in r:
            t.rs.append(o)
        for t in w:
            t.w = o; t.rs = []
        s.st[eng].append(o)
        return o

    def mm(s, out, lhsT, rhs, start, stop, r, w):
        return s.op("pe", lambda e: e.matmul(out, lhsT, rhs, start=start, stop=stop), r, w)

    def tr(s, out, in_, ident, r, w):
        return s.op("pe", lambda e: e.transpose(out, in_, ident), r, w)

    def act(s, out, in_, func, r, w, bias=None, scale=None):
        kw = {}
        if bias is not None: kw["bias"] = bias
        if scale is not None: kw["scale"] = scale
        return s.op("act", lambda e: e.activation(out, in_, func, **kw), r, w)

    def acopy(s, out, in_, r, w):
        return s.op("act", lambda e: e.copy(out, in_), r, w)

    def tt(s, out, a, b, op, r, w):
        return s.op("dve", lambda e: e.tensor_tensor(out, a, b, op), r, w)

    def ts(s, out, a, s1, s2, op0, op1, r, w):
        return s.op("dve", lambda e: e.tensor_scalar(out, a, s1, s2, op0, op1), r, w)

    def stt(s, out, a, sc, b, op0, op1, r, w):
        return s.op("dve", lambda e: e.scalar_tensor_tensor(out, a, sc, b, op0, op1), r, w)

    def vcopy(s, out, in_, r, w):
        return s.op("dve", lambda e: e.tensor_copy(out, in_), r, w)

    def recip(s, out, in_, r, w):
        return s.op("dve", lambda e: e.reciprocal(out, in_), r, w)

    def memset(s, ap, val, w):
        return s.op("dve", lambda e: e.memset(ap, val), (), w)

    def dma(s, q, out, in_, r, w):
        return s.op(q, lambda e: e.dma_start(out=out, in_=in_), r, w, dma=True)

    def emit(s, nsem_dma=14):
        nc = s.nc
        with contextlib.ExitStack() as es:
            nep = {}
            for e in ENGS:
                c = 0
                for o in s.st[e]:
                    if (not o.dma) and o.need:
                        c += 1; o.sv = c
                nep[e] = c // EPOCH + 1
            esem = {e: [es.enter_context(nc.semaphore("s_%s%d" % (e, i))) for i in range(nep[e])] for e in ENGS}
            dsems = {e: [es.enter_context(nc.semaphore("d_%s%d" % (e, i))) for i in range(nsem_dma)] for e in DMAQ}
            tot = {e: [0] * nsem_dma for e in DMAQ}
            for e in DMAQ:
                i = 0
                for o in s.st[e]:
                    if not o.dma: continue
                    o.dsem = i % nsem_dma
                    tot[e][o.dsem] += 16
                    o.dval = tot[e][o.dsem]
                    i += 1
            block = es.enter_context(nc.Block())

            def run(e, eng):
                waited = {}
                for o in s.st[e]:
                    ws = []
                    if o.dma and o.dval > 16:
                        ws.append((("d", e, o.dsem), dsems[e][o.dsem], o.dval - 16))
                    for d in o.deps:
                        if d.dma:
                            ws.append((("d", d.eng, d.dsem), dsems[d.eng][d.dsem], d.dval))
                        else:
                            ep, v = divmod(d.sv - 1, EPOCH)
                            ws.append((("e", d.eng, ep), esem[d.eng][ep], v + 1))
                    for key, sem, val in ws:
                        if waited.get(key, 0) >= val: continue
                        waited[key] = val
                        eng.wait_ge(sem, val)
                    ins = o.fn(eng)
                    if o.dma:
                        ins.then_inc(dsems[e][o.dsem], 16)
                    elif o.need:
                        ins.then_inc(esem[e][(o.sv - 1) // EPOCH], 1)
                if e in DMAQ:
                    for i in range(nsem_dma):
                        if tot[e][i] > 0:
                            eng.wait_ge(dsems[e][i], tot[e][i])

            block.tensor(lambda t: run("pe", t))
            block.scalar(lambda t: run("act", t))
            block.vector(lambda t: run("dve", t))
            block.gpsimd(lambda t: run("pool", t))
            block.sync(lambda t: run("sp", t))


def bc(ap, shape, axis):
    return ap.unsqueeze(axis).to_broadcast(shape)


def build(DEPTH=DEPTH, NSA=NSA, stats=None, phases="mod,ffn1,m1,ssd,fnet,m3,ffn2"):
    phases = set(phases.split(","))
    NTOK = NPR + NSA; NT = NTOK // 512; NCH = NTOK // 128
    nc = bass.Bass("TRN2", target_bir_lowering=False)
    k = K(nc)
    es = contextlib.ExitStack()

    def din(name, shape, dt=F32):
        return nc.dram_tensor(name, list(shape), dt, kind="ExternalInput").ap()

    def dout(name, shape, dt=F32):
        return nc.dram_tensor(name, list(shape), dt, kind="ExternalOutput").ap()

    def dscr(name, shape, dt):
        return nc.dram_tensor(name, list(shape), dt).ap()

    xT = din("xT", [D, NTOK])
    cv = din("cv", [128, 16, 2])
    h0T = din("h0T", [DEPTH, 2, 128, 2048])
    w_mod = din("w_mod", [DEPTH, D, 9 * D])
    bmodT = din("bmodT", [DEPTH, 128, 144])
    normgT = din("normgT", [128, DEPTH, 6, 16])
    w1u = din("w_ffn1_up", [DEPTH, D, 2 * DFF])
    w1d = din("w_ffn1_down", [DEPTH, DFF, D])
    w_in = din("w_in", [DEPTH, D, INC])
    convwT = din("convwT", [128, DEPTH, 24, 5])
    convbT = din("convbT", [128, DEPTH, 24])
    alogT = din("alogT", [64, DEPTH])
    dtbT = din("dtbT", [64, DEPTH])
    dskB = din("dskB", [128, DEPTH, 2, 32])
    gssdT = din("gssdT", [128, DEPTH, 16])
    gsguT = din("gsguT", [128, DEPTH, 8])
    wspT = din("wspT", [128, DEPTH, 8, 128])
    bspB = din("bspB", [128, DEPTH, 8, 128])
    w_out = din("w_out", [DEPTH, 2 * D, D])
    w2u = din("w_ffn2_up", [DEPTH, D, 2 * DFF])
    w2d = din("w_ffn2_down", [DEPTH, DFF, D])
    cst = din("cst", [128, 6, 128])
    tabC = din("tabC", [128, 2, 512])
    tab256 = din("tab256", [128, 2, 512])
    tabL = din("tabL", [2, NSA, NSA])

    yT = dout("yT", [D, NTOK])
    nst = dout("nst", [DEPTH, 2, 2, 128, 2048])

    ZS = dscr("ZS", [D, NTOK], BF16)
    YC = dscr("YC", [2 * D, NTOK], BF16)
    XS = dscr("XS", [NCH, 128, 2048], BF16)
    BTK = dscr("BTK", [NCH, 128, 512], BF16)
    BT = dscr("BT", [128, 4, NTOK], BF16)
    CT = dscr("CT", [128, 4, NTOK], BF16)
    DTA = dscr("DTA", [NCH, 128, 128], F32)
    FCS = dscr("FCS", [NTOK, 2048], BF16)
    YTM = dscr("YTM", [NCH, 128, 2048], F32)

    TxT = T()
    TyT = Ts(NT)
    TZS = Ts(NT); TYC = Ts(NT)
    TXSd = Ts(NCH); TBTKd = Ts(NCH); TBTd = Ts(NT); TCTd = Ts(NT); TDTAd = Ts(NCH); TFCSd = Ts(NT); TYTMd = Ts(NCH)
    TYCs = Ts(NCH)
    TYCf = Ts(NT)
    Tnst = T()

    def sb(name, shape, dt):
        return es.enter_context(nc.sbuf_tensor(name, list(shape), dt))

    XT = sb("XT", [128, 16, 512], F32); TXT = Ts(16)
    H = sb("H", [128, 16, 512], BF16); TH = Ts(16)
    HID = sb("HID", [128, 44, 512], BF16); THID = Ts(44)
    FO = sb("FO", [128, 16, 512], F32); TFO = Ts(16)
    WS = [sb("WS%d" % i, [128, 11264], BF16) for i in range(2)]
    TWS = [Ts(2) for _ in range(2)]
    CST = sb("CST", [128, 6, 128], F32); TC = T()
    IDB = sb("IDB", [128, 128], BF16)
    ONESB = sb("ONESB", [128, 128], BF16)
    EPSB = sb("EPSB", [128, 1], F32)
    MOD = sb("MOD", [128, 144, 2], F32); TMOD = T()
    SA = [sb("SA%d" % i, [128, 16, 2], F32) for i in range(3)]
    GB = [sb("GB%d" % i, [128, 16, 2], F32) for i in range(3)]
    CV = sb("CV", [128, 16, 2], F32)
    SIL = sb("SIL", [128, 16, 2], BF16); TSIL = T()
    BMOD = sb("BMOD", [128, DEPTH, 144], F32)
    NG = sb("NG", [128, DEPTH, 6, 16], F32)
    CW = sb("CW", [128, DEPTH, 24, 5], F32)
    CBI = sb("CBI", [128, DEPTH, 24], F32)
    ALOG = sb("ALOG", [64, DEPTH], F32)
    ANEG = sb("ANEG", [64, DEPTH], F32)
    DTB = sb("DTB", [64, DEPTH], F32)
    DSK = sb("DSK", [128, DEPTH, 2, 32], F32)
    DSUM = sb("DSUM", [128, DEPTH, 32], F32)
    GSS = sb("GSS", [128, DEPTH, 16], F32)
    GSG = sb("GSG", [128, DEPTH, 8], F32)
    WSP = sb("WSP", [128, 8, 128], BF16); TWSP = T()
    BSP = sb("BSP", [128, 8, 128], F32)
    TABC = sb("TABC", [128, 2, 512], BF16)
    TAB256 = sb("TAB256", [128, 2, 512], BF16)
    RS = sb("RS", [128, 512], F32); TRS = T()
    TMP = [sb("TMP%d" % i, [128, 512], F32) for i in range(3)]; TTMP = Ts(3)
    SQ = [sb("SQ%d" % i, [128, 512], BF16) for i in range(2)]; TSQ = Ts(2)
    ST = [sb("ST%d" % i, [128, 512], BF16) for i in range(3)]; TST = Ts(3)
    DTT = TMP[0][0:64, :]; TDTT = TTMP[0]
    ATT = TMP[1][0:64, :]; TATT = TTMP[1]
    DTAst = TMP[2][:].rearrange("p (c f) -> p c f", c=4); TDTAst = TTMP[2]
    SM = sb("SM", [128, 8, 32], F32); TSM = Ts(8)
    VTK = [sb("VTK%d" % i, [128, 4, 128], BF16) for i in range(2)]; TVTK = Ts(2)

    P = [es.enter_context(nc.psum_tensor("P%d" % i, [128, 512], F32)) for i in range(8)]
    TP = Ts(8)
    PB = [P[6][:].bitcast(BF16)[:, 0:512], P[7][:].bitcast(BF16)[:, 0:512]]
    TPB = [TP[6], TP[7]]

    const_T = T()

    cl = [const_T]
    k.dma("sp", CST[:], cst, (), cl)
    k.dma("sp", CV[:], cv, (), cl)
    k.dma("sp", BMOD[:], bmodT.rearrange("l p c -> p l c"), (), cl)
    k.dma("sp", NG[:], normgT, (), cl)
    k.dma("sp", CW[:], convwT, (), cl)
    k.dma("sp", CBI[:], convbT, (), cl)
    k.dma("sp", ALOG[:], alogT, (), cl)
    k.dma("sp", DTB[:], dtbT, (), cl)
    k.dma("sp", DSK[:], dskB, (), cl)
    k.dma("sp", GSS[:], gssdT, (), cl)
    k.dma("sp", GSG[:], gsguT, (), cl)
    k.dma("pool", TABC[:], tabC, (), cl)
    k.dma("pool", TAB256[:], tab256, (), cl)
    k.vcopy(IDB[:], CST[:, 0, :], cl, cl)
    k.vcopy(ONESB[:], CST[:, 1, :], cl, cl)
    k.memset(EPSB[:], EPS, cl)
    k.act(ANEG[:], ALOG[:], AF.Exp, cl, cl)
    k.ts(ANEG[:], ANEG[:], -1.0, 0.0, ALU.mult, ALU.add, cl, cl)
    k.tt(DSUM[:], DSK[:, :, 0, :], DSK[:, :, 1, :], ALU.add, cl, cl)
    k.act(SIL[:], CV[:], AF.Silu, cl, [TSIL])
    IDF = CST[:, 0, :]; ONESF = CST[:, 1, :]
    TRI = [CST[:, 2, :], CST[:, 3, :]]
    STRICT = [CST[:, 4, :], CST[:, 5, :]]

    state = {"ws": 0, "bank": 0, "st": 0, "pb": 0, "vtk": 0}

    def next_st():
        i = state["st"]; state["st"] = (i + 1) % 3
        return ST[i], TST[i]

    def next_pb():
        i = state["pb"]; state["pb"] = (i + 1) % 2
        return PB[i], TPB[i]

    def gemm(Wd, KC, blocks, rhs_fn, cb, out_fn=None):
        cid = 0
        for blk in blocks:
            i = state["ws"]; state["ws"] = 1 - i
            width = sum(n for _, n in blk)
            view = WS[i][:, :KC * width].rearrange("p (k c) -> p k c", k=KC)
            off = 0
            for j, (c0, n) in enumerate(blk):
                wl = [TWS[i][j]] if len(blk) == 2 else TWS[i]
                k.dma("pool", view[:, :, off:off + n], Wd[:, c0:c0 + n].rearrange("(k p) c -> p k c", p=128), (), wl)
                off += n
            c = 0
            while c < width:
                m = min(128, width - c)
                if out_fn is None:
                    b = state["bank"]; state["bank"] = (b + 1) % 4
                    ps, Tps = P[b][:m, :], TP[b]
                else:
                    ps, Tps = out_fn(cid)
                for kc in range(KC):
                    rhs, Trhs = rhs_fn(kc)
                    k.mm(ps, view[:, kc, c:c + m], rhs, kc == 0, kc == KC - 1, TWS[i] + [Trhs], [Tps])
                cb(cid, ps, Tps, m)
                cid += 1
                c += m

    def sumsq_rstd(get, Tl, n, dim):
        for i in range(n):
            j = i % 2
            k.act(SQ[j][:], get(i), AF.Square, [Tl[i]], [TSQ[j]])
            k.mm(P[4][:], ONESB[:], SQ[j][:], i == 0, i == n - 1, [TSQ[j], const_T], [TP[4]])
        k.act(RS[:], P[4][:], AF.Sqrt, [TP[4], const_T], [TRS], bias=EPSB[:, 0:1], scale=1.0 / dim)
        k.recip(RS[:], RS[:], [TRS], [TRS])

    def load_x(tile, src, Tsrc):
        c0 = tile * 512
        k.dma("sp", XT[:], src[:, c0:c0 + 512].rearrange("(k p) t -> p k t", p=128), [Tsrc], TXT)

    def prologue(tile, s, src, Tsrc):
        v = 0 if tile == 0 else 1
        load_x(tile, src, Tsrc)
        sumsq_rstd(lambda fc: XT[:, fc, :], TXT, 16, D)
        for fc in range(16):
            j = fc % 2
            k.stt(TMP[j][:], XT[:, fc, :], SA[s][:, fc, v:v + 1], RS[:], ALU.mult, ALU.mult, [TXT[fc], TRS, TMOD], [TTMP[j]])
            k.act(H[:, fc, :], TMP[j][:], AF.Identity, [TTMP[j], TMOD], [TH[fc]], bias=MOD[:, 48 * s + fc, v:v + 1])

    def epilogue(tile, s):
        v = 0 if tile == 0 else 1
        c0 = tile * 512
        sumsq_rstd(lambda fc: FO[:, fc, :], TFO, 16, D)
        for fc in range(16):
            j = fc % 2
            k.tt(TMP[j][:], FO[:, fc, :], RS[:], ALU.mult, [TFO[fc], TRS], [TTMP[j]])
            k.stt(XT[:, fc, :], TMP[j][:], GB[s][:, fc, v:v + 1], XT[:, fc, :], ALU.mult, ALU.add, [TTMP[j], TMOD, TXT[fc]], [TXT[fc]])
        k.dma("sp", yT[:, c0:c0 + 512].rearrange("(k p) t -> p k t", p=128), XT[:], TXT, [TyT[tile]])

    def mod_phase(l):
        def out_fn(cid):
            return P[5][:, cid * 2:cid * 2 + 2], TP[5]
        gemm(w_mod[l], 16, [[(cb * 512, 512)] for cb in range(36)], lambda kc: (SIL[:, kc, :], TSIL), lambda *a: None, out_fn)
        k.tt(MOD[:], P[5][:, 0:288].rearrange("p (c v) -> p c v", v=2), bc(BMOD[:, l, :], [128, 144, 2], 2), ALU.add, [TP[5], const_T], [TMOD])
        for s in range(3):
            coef = 1.0 if s == 1 else 0.5
            k.stt(SA[s][:], MOD[:, 48 * s + 16:48 * s + 32, :], 1.0, bc(NG[:, l, 2 * s, :], [128, 16, 2], 2), ALU.add, ALU.mult, [TMOD, const_T], [TMOD])
            k.stt(GB[s][:], MOD[:, 48 * s + 32:48 * s + 48, :], coef, bc(NG[:, l, 2 * s + 1, :], [128, 16, 2], 2), ALU.mult, ALU.mult, [TMOD, const_T], [TMOD])

    GT = [TMP[0], TMP[1]]; TGT = [TTMP[0], TTMP[1]]

    def ffn(l, Wu, Wd, s, tile, src, Tsrc):
        prologue(tile, s, src, Tsrc)

        def cb_up(cid, ps, Tps, m):
            b, j = divmod(cid, 4)
            if j < 2:
                k.act(GT[j][:], ps, AF.Silu, [Tps], [TGT[j]])
            else:
                jj = b * 2 + (j - 2)
                k.tt(HID[:, jj, :], GT[j - 2][:], ps, ALU.mult, [TGT[j - 2], Tps], [THID[jj]])
        gemm(Wu[l], 16, [[(b * 256, 256), (DFF + b * 256, 256)] for b in range(22)], lambda kc: (H[:, kc, :], TH[kc]), cb_up)

        def cb_dn(cid, ps, Tps, m):
            k.acopy(FO[:, cid, :], ps, [Tps], [TFO[cid]])
        gemm(Wd[l], 44, [[(b * 256, 256)] for b in range(8)], lambda kc: (HID[:, kc, :], THID[kc]), cb_dn)
        epilogue(tile, s)

    FT = HID[:, 0:8, :]; TFT = THID[0:8]
    UT = HID[:, 8:16, :]; TUT = THID[8:16]
    VT = HID[:, 16:24, :]; TVT = THID[16:24]
    flat = lambda ap: ap.rearrange("p a f -> p (a f)")
    XSst = flat(HID[:, 24:40, :]).rearrange("p (c f) -> p c f", c=4)
    TXSst = THID[24:40]
    BTKst = flat(HID[:, 40:44, :]).rearrange("p (c f) -> p c f", c=4)
    TBTKst = THID[40:44]
    FCst = flat(FO[:, 0:8, :]).bitcast(BF16).rearrange("p (c f) -> p c f", c=4)
    TFCst = TFO[0:8]

    def m1(l, tile):
        c0 = tile * 512
        prologue(tile, 1, yT, TyT[tile])
        R_, L_ = (2, 256) if tile == 0 else (8, 64)
        blocks = [[(i * 512, 512)] for i in range(4)] + [[(2048 + i * 512, 512)] for i in range(6)] + [[(5120, 64)]] \
            + [[(5184 + i * 512, 512)] for i in range(2)] + [[(6208 + i * 512, 512)] for i in range(4)]

        def cb(cid, ps, Tps, m):
            if cid < 16:
                st, Tst = next_st()
                k.act(st[:], ps, AF.Silu, [Tps], [Tst])
                k.dma("sp", ZS[cid * 128:(cid + 1) * 128, c0:c0 + 512], st[:], [Tst], [TZS[tile]])
            elif cid < 40:
                xc = cid - 16
                acc = TMP[2]; Tacc = TTMP[2]
                k.ts(acc[:], ps, CW[:, l, xc, 2:3], CBI[:, l, xc:xc + 1], ALU.mult, ALU.add, [Tps, const_T], [Tacc])
                accv = acc[:].rearrange("p (r q) -> p r q", r=R_)
                psv = ps.rearrange("p (r q) -> p r q", r=R_)
                for kk, d in ((0, -2), (1, -1), (3, 1), (4, 2)):
                    lo, hi = max(0, -d), L_ - max(0, d)
                    k.stt(accv[:, :, lo:hi], psv[:, :, lo + d:hi + d], CW[:, l, xc, kk:kk + 1], accv[:, :, lo:hi], ALU.mult, ALU.add,
                          [Tps, const_T, Tacc], [Tacc])
                st, Tst = next_st()
                k.act(st[:], acc[:], AF.Silu, [Tacc], [Tst])
                if xc < 16 or xc < 20:
                    pb, Tpb = next_pb()
                    for c in range(4):
                        k.tr(pb[:, c * 128:(c + 1) * 128], st[:, c * 128:(c + 1) * 128], IDB[:], [Tst, const_T], [Tpb])
                    if xc < 16:
                        k.vcopy(XSst[:, :, xc * 128:(xc + 1) * 128], pb.rearrange("p (c f) -> p c f", c=4), [Tpb], TXSst)
                        if xc == 15:
                            k.dma("sp", XS[tile * 4:(tile + 1) * 4].rearrange("c p f -> p c f"), XSst, TXSst, TXSd[tile * 4:(tile + 1) * 4])
                    else:
                        g = xc - 16
                        k.dma("sp", BT[:, g, c0:c0 + 512], st[:], [Tst], [TBTd[tile]])
                        k.vcopy(BTKst[:, :, g * 128:(g + 1) * 128], pb.rearrange("p (c f) -> p c f", c=4), [Tpb], TBTKst)
                        if g == 3:
                            k.dma("sp", BTK[tile * 4:(tile + 1) * 4].rearrange("c p f -> p c f"), BTKst, TBTKst, TBTKd[tile * 4:(tile + 1) * 4])
                else:
                    g = xc - 20
                    k.dma("sp", CT[:, g, c0:c0 + 512], st[:], [Tst], [TCTd[tile]])
            elif cid == 40:
                k.act(ATT, ps, AF.Exp, [Tps, const_T], [TATT], bias=DTB[:, l:l + 1])
                k.act(DTT, ATT, AF.Ln, [TATT], [TDTT], bias=1.0)
                k.ts(ATT, DTT, ANEG[:, l:l + 1], 0.0, ALU.mult, ALU.add, [TDTT, const_T], [TATT])
                for c in range(4):
                    k.tr(P[4][:, c * 128:c * 128 + 64], DTT[:, c * 128:(c + 1) * 128], IDF[0:64, 0:64], [TDTT, const_T], [TP[4]])
                    k.tr(P[4][:, c * 128 + 64:c * 128 + 128], ATT[:, c * 128:(c + 1) * 128], IDF[0:64, 0:64], [TATT, const_T], [TP[4]])
                k.vcopy(DTAst, P[4][:].rearrange("p (c f) -> p c f", c=4), [TP[4]], [TDTAst])
                k.dma("sp", DTA[tile * 4:(tile + 1) * 4].rearrange("c p f -> p c f"), DTAst, [TDTAst], TDTAd[tile * 4:(tile + 1) * 4])
            elif cid < 49:
                fi = cid - 41
                k.acopy(FT[:, fi, :], ps, [Tps], [TFT[fi]])
                if fi == 7:
                    for c in range(4):
                        for g in range(4):
                            for hf in range(2):
                                k.mm(P[5][:], FT[:, 2 * g + hf, c * 128:(c + 1) * 128], TABC[:, hf, :], hf == 0, hf == 1,
                                     [TFT[2 * g + hf], const_T], [TP[5]])
                            dst = FCst[:, c, :].rearrange("p (two g q) -> p two g q", two=2, g=4)[:, :, g, :]
                            k.vcopy(dst, P[5][:].rearrange("p (two q) -> p two q", two=2), [TP[5]], TFCst)
                    if tile == 0:
                        for sq in range(2):
                            for j in range(8):
                                n = 0
                                for part in range(2):
                                    for cl_ in range(2):
                                        k.mm(P[5][:, 0:256], FCst[:, 2 * sq + cl_, part * 1024 + j * 128:part * 1024 + (j + 1) * 128],
                                             TAB256[:, cl_, part * 256:(part + 1) * 256], n == 0, n == 3, TFCst + [const_T], [TP[5]])
                                        n += 1
                                st, Tst = next_st()
                                k.acopy(st[:, 0:256], P[5][:, 0:256], [TP[5]], [Tst])
                                k.dma("sp", YC[2048 + j * 128:2048 + (j + 1) * 128, sq * 256:(sq + 1) * 256], st[:, 0:256], [Tst], [TYCf[0]])
                    else:
                        k.dma("sp", FCS[c0:c0 + 512].rearrange("(c p) f -> p c f", p=128), FCst, TFCst, [TFCSd[tile]])
            else:
                si = cid - 49
                A, TA = TMP[0], TTMP[0]
                B, TB = TMP[1], TTMP[1]
                k.acopy(A[:], ps, [Tps], [TA])
                k.tt(B[:], A[:], A[:], ALU.mult, [TA], [TB])
                k.ts(B[:], B[:], 0.044715, 1.0, ALU.mult, ALU.add, [TB], [TB])
                k.tt(B[:], B[:], A[:], ALU.mult, [TB, TA], [TB])
                k.act(B[:], B[:], AF.Sigmoid, [TB], [TB], scale=1.5957691216057308)
                if si < 8:
                    k.tt(UT[:, si, :], A[:], B[:], ALU.mult, [TA, TB], [TUT[si]])
                else:
                    h = si - 8
                    k.tt(VT[:, h, :], A[:], B[:], ALU.mult, [TA, TB], [TVT[h]])
                    if h == 7:
                        sumsq_rstd(lambda i: VT[:, i, :], TVT, 8, 1024)
                        for hh in range(8):
                            k.stt(VT[:, hh, :], VT[:, hh, :], GSG[:, l, hh:hh + 1], RS[:], ALU.mult, ALU.mult, [TVT[hh], TRS, const_T], [TVT[hh]])
                            pb, Tpb = next_pb()
                            for c in range(4):
                                k.tr(pb[:, c * 128:(c + 1) * 128], VT[:, hh, c * 128:(c + 1) * 128], IDB[:], [TVT[hh], const_T], [Tpb])
                            vi = state["vtk"]; state["vtk"] = 1 - vi
                            k.vcopy(VTK[vi][:], pb.rearrange("p (c f) -> p c f", c=4), [Tpb], [TVTK[vi]])
                            for c in range(4):
                                k.mm(P[5][:, c * 128:(c + 1) * 128], VTK[vi][:, c, :], WSP[:, hh, :], True, True, [TVTK[vi], TWSP], [TP[5]])
                            k.tt(A[:].rearrange("p (c t) -> p c t", c=4), P[5][:].rearrange("p (c t) -> p c t", c=4),
                                 bc(BSP[:, hh, :], [128, 4, 128], 1), ALU.add, [TP[5], TWSP], [TA])
                            st, Tst = next_st()
                            k.tt(st[:], A[:], UT[:, hh, :], ALU.mult, [TA, TUT[hh]], [Tst])
                            k.dma("sp", YC[3072 + hh * 128:3072 + (hh + 1) * 128, c0:c0 + 512], st[:], [Tst], [TYC[tile]])
        gemm(w_in[l], 16, blocks, lambda kc: (H[:, kc, :], TH[kc]), cb)

    HS = flat(XT[:, 0:4, :]); THS = TXT[0:4]
    YA = flat(XT[:, 4:8, :]); TYA = TXT[4:8]
    YF = flat(XT[:, 8:12, :]); TYF = TXT[8:12]
    RBUF = flat(FO[:, 0:8, :]).rearrange("p (h t) -> p h t", h=32); TRB = TFO[0:8]
    CBM = FO[:, 8, :].rearrange("p (g t) -> p g t", g=4); TCBM = [TFO[8]]
    LT = [FO[:, 9 + i, 0:128] for i in range(3)]; TLT = [[TFO[9 + i]] for i in range(3)]
    TMPA = FO[:, 12, :]; TTA = [TFO[12]]
    TMPB = FO[:, 13, :]; TTB = [TFO[13]]
    TMPC = FO[:, 14, :]; TTC = [TFO[14]]
    HB = flat(H[:, 0:4, :]); THB = TH[0:4]
    XSb = [flat(H[:, 4:8, :]), flat(H[:, 8:12, :])]; TXSb = [TH[4:8], TH[8:12]]
    XD = flat(H[:, 12:16, :]); TXD = TH[12:16]
    XW = flat(HID[:, 0:4, :]); TXW = THID[0:4]
    BTKb = [HID[:, 4, :], HID[:, 5, :]]; TBTKb = [[THID[4]], [THID[5]]]
    BTb = [HID[:, 6, :].rearrange("p (g t) -> p g t", g=4), HID[:, 7, :].rearrange("p (g t) -> p g t", g=4)]; TBTb = [[THID[6]], [THID[7]]]
    CTb = [HID[:, 8, :].rearrange("p (g t) -> p g t", g=4), HID[:, 9, :].rearrange("p (g t) -> p g t", g=4)]; TCTb = [[THID[8]], [THID[9]]]
    MT = [HID[:, 10 + i, 0:128] for i in range(3)]; TMT = [[THID[10 + i]] for i in range(3)]
    YAb = flat(HID[:, 16:20, :]); TYAb = THID[16:20]
    YTst = flat(HID[:, 20:24, :]).rearrange("p (fc t) -> p fc t", fc=16); TYTst = THID[20:24]
    DTAb = [TMP[0][:, 0:128], TMP[1][:, 0:128]]; TDTAb = [[TTMP[0]], [TTMP[1]]]
    TD = Ts(8)
    SMT = [[t] for t in TSM]

    def ssd_seq(l, chunks, h0, final):
        for d in range(2):
            order = chunks if d == 0 else chunks[::-1]
            if h0 is not None:
                k.dma("sp", HS, h0[l, d], [const_T], THS)
                k.acopy(HB, HS, THS, THB)
            else:
                k.memset(HS, 0.0, THS)
                k.memset(HB, 0.0, THB)
            for ci, c in enumerate(order):
                q = ci % 2
                tile = c // 4
                k.dma("sp", XSb[q], XS[c], [TXSd[c]], TXSb[q])
                k.dma("sp", BTKb[q], BTK[c], [TBTKd[c]], TBTKb[q])
                k.dma("sp", BTb[q], BT[:, :, c * 128:(c + 1) * 128], [TBTd[tile]], TBTb[q])
                k.dma("sp", CTb[q], CT[:, :, c * 128:(c + 1) * 128], [TCTd[tile]], TCTb[q])
                k.dma("sp", DTAb[q], DTA[c], [TDTAd[c]], TDTAb[q])
                if d == 1:
                    k.dma("sp", YF, YTM[c], [TYTMd[c]], TYF)
                dt_d = DTAb[q][:, 32 * d:32 * d + 32]
                a_d = DTAb[q][:, 64 + 32 * d:96 + 32 * d]
                CS, ECS, DTE, CD, W1 = SM[:, 0, :], SM[:, 1, :], SM[:, 2, :], SM[:, 3, :], SM[:, 4, :]
                k.mm(P[5][:, 0:32], TRI[d], a_d, True, True, [const_T] + TDTAb[q], [TP[5]])
                k.mm(P[5][:, 32:64], ONESF, a_d, True, True, [const_T] + TDTAb[q], [TP[5]])
                k.acopy(CS, P[5][:, 0:32], [TP[5]], [TSM[0]])
                k.act(ECS, P[5][:, 0:32], AF.Exp, [TP[5]], [TSM[1]])
                k.act(CD, P[5][:, 32:64], AF.Exp, [TP[5]], [TSM[3]])
                k.tt(DTE, P[5][:, 32:64], CS, ALU.subtract, [TP[5], TSM[0]], [TSM[2]])
                k.act(DTE, DTE, AF.Exp, [TSM[2]], [TSM[2]])
                k.tt(W1, dt_d, DTE, ALU.mult, TDTAb[q] + [TSM[2]], [TSM[4]])
                xs3 = XSb[q].rearrange("p (h e) -> p h e", h=32)
                k.tt(XW.rearrange("p (h e) -> p h e", h=32), xs3, bc(W1, [128, 32, 64], 2), ALU.mult, TXSb[q] + [TSM[4]], TXW)
                k.tt(XD.rearrange("p (h e) -> p h e", h=32), xs3, bc(dt_d, [128, 32, 64], 2), ALU.mult, TXSb[q] + TDTAb[q], TXD)
                k.tt(RBUF, bc(TRI[d], [128, 32, 128], 1), bc(a_d, [128, 32, 128], 2), ALU.mult, [const_T] + TDTAb[q], TRB)
                for g in range(4):
                    k.mm(P[0][:, g * 128:(g + 1) * 128], BTb[q][:, g, :], CTb[q][:, g, :], True, True, TBTb[q] + TCTb[q], [TP[0]])
                k.tt(CBM, P[0][:].rearrange("p (g t) -> p g t", g=4), bc(TRI[d], [128, 4, 128], 1), ALU.mult, [TP[0], const_T], TCBM)
                hq = 0
                for g in range(4):
                    gs = slice(g * 512, (g + 1) * 512)
                    for r in range(8):
                        hh = 8 * g + r
                        dq = hq % 2; lq = hq % 3; hq += 1
                        Dps = P[1 + dq][:, 0:128]
                        k.mm(Dps, STRICT[d], RBUF[:, hh, :], True, True, [const_T] + TRB, [TP[1 + dq]])
                        k.act(LT[lq], Dps, AF.Exp, [TP[1 + dq]], TLT[lq])
                        k.tt(MT[lq], LT[lq], CBM[:, g, :], ALU.mult, TLT[lq] + TCBM, TMT[lq])
                        k.mm(P[3][:, r * 64:(r + 1) * 64], MT[lq], XD[:, hh * 64:(hh + 1) * 64], True, True, TMT[lq] + TXD, [TP[3]])
                    k.mm(P[4][:], CTb[q][:, g, :], HB[:, gs], True, True, TCTb[q] + THB, [TP[4]])
                    k.mm(P[5][:], BTKb[q][:, g * 128:(g + 1) * 128], XW[:, gs], True, True, TBTKb[q] + TXW, [TP[5]])
                    v3 = lambda ap: ap.rearrange("p (h e) -> p h e", h=8)
                    k.tt(v3(TMPA), v3(P[4][:]), bc(ECS[:, 8 * g:8 * g + 8], [128, 8, 64], 2), ALU.mult, [TP[4], TSM[1]], TTA)
                    k.tt(YA[:, gs], P[3][:], TMPA, ALU.add, [TP[3]] + TTA, TYA)
                    if d == 0:
                        k.tt(v3(TMPB), v3(XSb[q][:, gs]), bc(DSUM[:, l, 8 * g:8 * g + 8], [128, 8, 64], 2), ALU.mult, TXSb[q] + [const_T], TTB)
                        k.tt(YA[:, gs], YA[:, gs], TMPB, ALU.add, TYA + TTB, TYA)
                    else:
                        k.tt(YA[:, gs], YA[:, gs], YF[:, gs], ALU.add, TYA + TYF, TYA)
                    k.tt(v3(TMPC), v3(HS[:, gs]), bc(CD[:, 8 * g:8 * g + 8], [128, 8, 64], 2), ALU.mult, THS + [TSM[3]], TTC)
                    k.tt(HS[:, gs], TMPC, P[5][:], ALU.add, TTC + [TP[5]], THS)
                    k.acopy(HB[:, gs], HS[:, gs], THS, THB)
                if d == 0:
                    k.dma("sp", YTM[c], YA, TYA, [TYTMd[c]])
                else:
                    k.acopy(YAb, YA, TYA, TYAb)
                    for f4 in range(4):
                        pb, Tpb = next_pb()
                        for j in range(4):
                            fc = f4 * 4 + j
                            k.tr(pb[:, j * 128:(j + 1) * 128], YAb[:, fc * 128:(fc + 1) * 128], IDB[:], TYAb + [const_T], [Tpb])
                        k.vcopy(YTst[:, f4 * 4:(f4 + 1) * 4, :], pb.rearrange("p (j t) -> p j t", j=4), [Tpb], TYTst)
                    k.dma("sp", YC[0:2048, c * 128:(c + 1) * 128].rearrange("(fc p) t -> p fc t", p=128), YTst, TYTst, [TYCs[c]])
            if final is not None:
                k.dma("sp", final[d], HS, THS, [Tnst])

    FCb = [H[:, 4 * i:4 * i + 4, :].rearrange("p a f -> p (a f)") for i in range(4)]; TFCb = [TH[0], TH[4], TH[8], TH[12]]

    def fnet_sample(l):
        for kb in range(NSA // 512):
            for lc in range(NSA // 128):
                if lc % 4 == 0:
                    i = state["ws"]; state["ws"] = 1 - i
                    slab = WS[i][:, 0:4096].rearrange("p (tb c q) -> p tb c q", tb=2, c=4)
                    for tb in range(2):
                        k.dma("pool", slab[:, tb], tabL[tb, lc * 128:(lc + 4) * 128, kb * 512:(kb + 1) * 512].rearrange("(c p) q -> p c q", p=128),
                              (), [TWS[i][tb]])
                    cur = (slab, TWS[i])
                slab, Tsl = cur
                fq = lc % 4
                row0 = NPR + lc * 128
                k.dma("sp", FCb[fq], FCS[row0:row0 + 128, :], [TFCSd[row0 // 512]], [TFCb[fq]])
                for j in range(8):
                    k.mm(P[j][:], FCb[fq][:, j * 128:(j + 1) * 128], slab[:, 0, lc % 4, :], lc == 0, False, [TFCb[fq]] + Tsl, [TP[j]])
                    k.mm(P[j][:], FCb[fq][:, 1024 + j * 128:1024 + (j + 1) * 128], slab[:, 1, lc % 4, :], False, lc == NSA // 128 - 1, [TFCb[fq]] + Tsl, [TP[j]])
            for j in range(8):
                st, Tst = next_st()
                k.acopy(st[:], P[j][:], [TP[j]], [Tst])
                k.dma("sp", YC[2048 + j * 128:2048 + (j + 1) * 128, NPR + kb * 512:NPR + (kb + 1) * 512], st[:], [Tst], [TYCf[kb + 1]])

    def m3(l, tile):
        c0 = tile * 512
        deps = [TYC[tile], TYCf[tile]] + TYCs[tile * 4:(tile + 1) * 4]
        k.dma("sp", HID[:, 0:32, :], YC[:, c0:c0 + 512].rearrange("(k p) t -> p k t", p=128), deps, THID[0:32])
        k.dma("sp", H[:], ZS[:, c0:c0 + 512].rearrange("(k p) t -> p k t", p=128), [TZS[tile]], TH)
        for fc in range(16):
            k.tt(FO[:, fc, :], HID[:, fc, :], H[:, fc, :], ALU.mult, [THID[fc], TH[fc]], [TFO[fc]])
        sumsq_rstd(lambda fc: FO[:, fc, :], TFO, 16, D)
        for fc in range(16):
            k.stt(HID[:, fc, :], FO[:, fc, :], GSS[:, l, fc:fc + 1], RS[:], ALU.mult, ALU.mult, [TFO[fc], TRS, const_T], [THID[fc]])

        def cb(cid, ps, Tps, m):
            k.acopy(FO[:, cid, :], ps, [Tps], [TFO[cid]])
        gemm(w_out[l], 32, [[(b * 256, 256)] for b in range(8)], lambda kc: (HID[:, kc, :], THID[kc]), cb)
        load_x(tile, yT, TyT[tile])
        epilogue(tile, 1)

    for l in range(DEPTH):
        k.dma("pool", WSP[:], wspT[:, l], (), [TWSP])
        k.dma("sp", BSP[:], bspB[:, l], (), [TWSP])
        if "mod" in phases:
            mod_phase(l)
        if "ffn1" in phases:
            for t in range(NT):
                if l == 0:
                    ffn(l, w1u, w1d, 0, t, xT, TxT)
                else:
                    ffn(l, w1u, w1d, 0, t, yT, TyT[t])
        if "m1" in phases:
            for t in range(NT):
                m1(l, t)
        if "ssd" in phases:
            ssd_seq(l, [0, 1], None, [nst[l, 0, 0], nst[l, 0, 1]])
            ssd_seq(l, [2, 3], None, [nst[l, 1, 0], nst[l, 1, 1]])
            ssd_seq(l, list(range(4, NCH)), h0T, None)
        if "fnet" in phases:
            fnet_sample(l)
        if "m3" in phases:
            for t in range(NT):
                m3(l, t)
        if "ffn2" in phases:
            for t in range(NT):
                ffn(l, w2u, w2d, 2, t, yT, TyT[t])
    if stats is not None:
        stats.update({e: len(k.st[e]) for e in ENGS})
        stats.update({e + '_need': sum(1 for o in k.st[e] if o.need and not o.dma) for e in ENGS})
    k.emit()
    es.close()
    return nc


_CACHE = {}


def _consts(NSA=NSA):
    i = np.arange(128)
    ident = (i[:, None] == i[None, :]).astype(np.float32)
    ones = np.ones((128, 128), np.float32)
    le = (i[:, None] <= i[None, :]).astype(np.float32)
    ge = (i[:, None] >= i[None, :]).astype(np.float32)
    gt = (i[:, None] > i[None, :]).astype(np.float32)
    lt = (i[:, None] < i[None, :]).astype(np.float32)
    cst = np.stack([ident, ones, le, ge, gt, lt], axis=1).astype(np.float32)
    c = np.arange(256)
    ang = 2 * np.pi * np.outer(c, c) / 256.0
    cosC = np.cos(ang) / 16.0; sinC = np.sin(ang) / 16.0
    tabC = np.concatenate([cosC, sinC], axis=1).reshape(2, 128, 512).transpose(1, 0, 2)
    tab256 = np.concatenate([cosC, -sinC], axis=1).reshape(2, 128, 512).transpose(1, 0, 2)
    n = np.arange(NSA)
    lk = (np.outer(n, n) % NSA).astype(np.float64)
    angL = 2 * np.pi * lk / NSA
    sc = 1.0 / np.sqrt(float(NSA))
    tabL = np.stack([np.cos(angL) * sc, -np.sin(angL) * sc]).astype(np.float32)
    return (np.ascontiguousarray(cst), np.ascontiguousarray(tabC.astype(np.float32)),
            np.ascontiguousarray(tab256.astype(np.float32)), tabL)


def kernel(x_prompt, x_sample, state_ssd, c, c_ctx, w_mod, b_mod, norm_g, w_ffn1_up, w_ffn1_down, w_in, conv_w, conv_b,
           a_log, dt_bias, d_skip, g_ssd, g_sgu, w_sp, b_sp, w_out, w_ffn2_up, w_ffn2_down):
    f = lambda a: np.ascontiguousarray(np.asarray(a, dtype=np.float32))
    x_prompt, x_sample, state_ssd, c, c_ctx = map(f, (x_prompt, x_sample, state_ssd, c, c_ctx))
    if "nc" not in _CACHE:
        _CACHE["nc"] = build()
        _CACHE["consts"] = _consts()
    nc = _CACHE["nc"]
    cst, tabC, tab256, tabL = _CACHE["consts"]
    fm = lambda v, nch: f(np.asarray(v).reshape(nch, 128).T)
    shared = {
        "w_mod": f(w_mod), "w_ffn1_up": f(w_ffn1_up), "w_ffn1_down": f(w_ffn1_down), "w_in": f(w_in), "w_out": f(w_out),
        "w_ffn2_up": f(w_ffn2_up), "w_ffn2_down": f(w_ffn2_down),
        "bmodT": f(np.stack([fm(b_mod[l], 144) for l in range(DEPTH)])),
        "normgT": f(np.stack([np.stack([fm(norm_g[l][j], 16) for j in range(6)], axis=1) for l in range(DEPTH)], axis=1)),
        "convwT": f(np.stack([np.stack([fm(conv_w[l][kk], 24) for kk in range(5)], axis=2) for l in range(DEPTH)], axis=1)),
        "convbT": f(np.stack([fm(conv_b[l], 24) for l in range(DEPTH)], axis=1)),
        "alogT": f(np.asarray(a_log).reshape(DEPTH, 64).T), "dtbT": f(np.asarray(dt_bias).reshape(DEPTH, 64).T),
        "dskB": f(np.broadcast_to(np.asarray(d_skip)[None], (128, DEPTH, 2, 32))),
        "gssdT": f(np.stack([fm(g_ssd[l], 16) for l in range(DEPTH)], axis=1)),
        "gsguT": f(np.stack([fm(g_sgu[l], 8) for l in range(DEPTH)], axis=1)),
        "wspT": f(np.asarray(w_sp).transpose(3, 0, 1, 2)),
        "bspB": f(np.broadcast_to(np.asarray(b_sp)[None], (128, DEPTH, 8, 128))),
        "cst": cst, "tabC": tabC, "tab256": tab256, "tabL": tabL,
    }
    in_maps = []
    for i in range(8):
        s = i % 2
        xt = np.concatenate([x_prompt[2 * i:2 * i + 2].reshape(NPR, D), x_sample[s]], axis=0)
        m = dict(shared)
        m["xT"] = f(xt.T)
        m["cv"] = f(np.stack([fm(c_ctx, 16), fm(c[s], 16)], axis=2))
        m["h0T"] = f(state_ssd[s].transpose(0, 1, 4, 2, 3).reshape(DEPTH, 2, 128, 2048))
        in_maps.append(m)
    res = run_bass_kernel_spmd(nc, in_maps, core_ids=list(range(8)))
    outs = res.results
    y_prompt = np.empty((16, 256, D), np.float32)
    y_sample = np.empty((2, NSA, D), np.float32)
    new_state = np.empty((16, DEPTH, 2, 32, 64, 128), np.float32)
    for i in range(8):
        yt = np.asarray(outs[i]["yT"]).T
        y_prompt[2 * i:2 * i + 2] = yt[:NPR].reshape(2, 256, D)
        if i < 2:
            y_sample[i] = yt[NPR:]
        ns = np.asarray(outs[i]["nst"])
        new_state[2 * i:2 * i + 2] = ns.reshape(DEPTH, 2, 2, 128, 32, 64).transpose(1, 0, 2, 4, 5, 3)
    return (y_prompt, y_sample, new_state)
```

```python
import contextlib
import numpy as np
import concourse.bass as bass
import concourse.mybir as mybir
from concourse.bass_utils import run_bass_kernel_spmd

F32 = mybir.dt.float32
BF16 = mybir.dt.bfloat16
AF = mybir.ActivationFunctionType
ALU = mybir.AluOpType

D = 2048
DFF = 5632
NPR = 512
NSA = 4096
NTOK = NPR + NSA
NT = NTOK // 512
NCH = NTOK // 128
INC = 8256
EPS = 1e-6
DEPTH = 2
EPOCH = 12000
import os
SSD_STQ = os.environ.get('SSD_STQ', 'pool')
SSD_PF = int(os.environ.get('SSD_PF', '1'))
SSD_B4 = int(os.environ.get('SSD_B4', '0'))


class T:
    __slots__ = ("w", "rs")

    def __init__(s):
        s.w = None
        s.rs = []


def Ts(n):
    return [T() for _ in range(n)]


class Op:
    __slots__ = ("eng", "fn", "deps", "need", "sv", "dma", "dsem", "dval")

    def __init__(s, eng, fn, dma):
        s.eng = eng; s.fn = fn; s.deps = []; s.need = False; s.sv = 0; s.dma = dma; s.dsem = 0; s.dval = 0


ENGS = ("pe", "act", "dve", "pool", "sp")
DMAQ = ("pool", "sp")


class K:
    def __init__(s, nc):
        s.nc = nc
        s.st = {e: [] for e in ENGS}

    def op(s, eng, fn, r=(), w=(), dma=False):
        o = Op(eng, fn, dma)
        seen = set()
        for t in r:
            d = t.w
            if d is not None and id(d) not in seen:
                seen.add(id(d)); o.deps.append(d)
        for t in w:
            d = t.w
            if d is not None and id(d) not in seen:
                seen.add(id(d)); o.deps.append(d)
            last = {}
            for d in t.rs:
                last[d.eng if not d.dma else id(d)] = d
            for d in last.values():
                if id(d) not in seen:
                    seen.add(id(d)); o.deps.append(d)
        if eng == "pe":
            o.deps = [d for d in o.deps if not (d.eng == "pe" and not d.dma)]
        for d in o.deps:
            d.need = True
        for t in r:
            t.rs.append(o)
        for t in w:
            t.w = o; t.rs = []
        s.st[eng].append(o)
        return o

    def mm(s, out, lhsT, rhs, start, stop, r, w):
        return s.op("pe", lambda e: e.matmul(out, lhsT, rhs, start=start, stop=stop), r, w)

    def tr(s, out, in_, ident, r, w):
        return s.op("pe", lambda e: e.transpose(out, in_, ident), r, w)

    def act(s, out, in_, func, r, w, bias=None, scale=None):
        kw = {}
        if bias is not None: kw["bias"] = bias
        if scale is not None: kw["scale"] = scale
        return s.op("act", lambda e: e.activation(out, in_, func, **kw), r, w)

    def acopy(s, out, in_, r, w):
        return s.op("act", lambda e: e.copy(out, in_), r, w)

    def tt(s, out, a, b, op, r, w):
        return s.op("dve", lambda e: e.tensor_tensor(out, a, b, op), r, w)

    def ts(s, out, a, s1, s2, op0, op1, r, w):
        return s.op("dve", lambda e: e.tensor_scalar(out, a, s1, s2, op0, op1), r, w)

    def stt(s, out, a, sc, b, op0, op1, r, w):
        return s.op("dve", lambda e: e.scalar_tensor_tensor(out, a, sc, b, op0, op1), r, w)

    def vcopy(s, out, in_, r, w):
        return s.op("dve", lambda e: e.tensor_copy(out, in_), r, w)

    def recip(s, out, in_, r, w):
        return s.op("dve", lambda e: e.reciprocal(out, in_), r, w)

    def memset(s, ap, val, w):
        return s.op("dve", lambda e: e.memset(ap, val), (), w)

    def dma(s, q, out, in_, r, w):
        return s.op(q, lambda e: e.dma_start(out=out, in_=in_), r, w, dma=True)

    def emit(s, nsem_dma=14):
        nc = s.nc
        with contextlib.ExitStack() as es:
            nep = {}
            for e in ENGS:
                c = 0
                for o in s.st[e]:
                    if (not o.dma) and o.need:
                        c += 1; o.sv = c
                nep[e] = c // EPOCH + 1
            esem = {e: [es.enter_context(nc.semaphore("s_%s%d" % (e, i))) for i in range(nep[e])] for e in ENGS}
            dsems = {e: [es.enter_context(nc.semaphore("d_%s%d" % (e, i))) for i in range(nsem_dma)] for e in DMAQ}
            tot = {e: [0] * nsem_dma for e in DMAQ}
            for e in DMAQ:
                i = 0
                for o in s.st[e]:
                    if not o.dma: continue
                    o.dsem = i % nsem_dma
                    tot[e][o.dsem] += 16
                    o.dval = tot[e][o.dsem]
                    i += 1
            block = es.enter_context(nc.Block())

            def run(e, eng):
                waited = {}
                for o in s.st[e]:
                    ws = []
                    if o.dma and o.dval > 16:
                        ws.append((("d", e, o.dsem), dsems[e][o.dsem], o.dval - 16))
                    for d in o.deps:
                        if d.dma:
                            ws.append((("d", d.eng, d.dsem), dsems[d.eng][d.dsem], d.dval))
                        else:
                            ep, v = divmod(d.sv - 1, EPOCH)
                            ws.append((("e", d.eng, ep), esem[d.eng][ep], v + 1))
                    for key, sem, val in ws:
                        if waited.get(key, 0) >= val: continue
                        waited[key] = val
                        eng.wait_ge(sem, val)
                    ins = o.fn(eng)
                    if o.dma:
                        ins.then_inc(dsems[e][o.dsem], 16)
                    elif o.need:
                        ins.then_inc(esem[e][(o.sv - 1) // EPOCH], 1)
                if e in DMAQ:
                    for i in range(nsem_dma):
                        if tot[e][i] > 0:
                            eng.wait_ge(dsems[e][i], tot[e][i])

            block.tensor(lambda t: run("pe", t))
            block.scalar(lambda t: run("act", t))
            block.vector(lambda t: run("dve", t))
            block.gpsimd(lambda t: run("pool", t))
            block.sync(lambda t: run("sp", t))


def bc(ap, shape, axis):
    return ap.unsqueeze(axis).to_broadcast(shape)


def build(DEPTH=DEPTH, NSA=NSA, stats=None, phases="mod,ffn1,m1,ssd,fnet,m3,ffn2"):
    phases = set(phases.split(","))
    NTOK = NPR + NSA; NT = NTOK // 512; NCH = NTOK // 128
    nc = bass.Bass("TRN2", target_bir_lowering=False)
    k = K(nc)
    es = contextlib.ExitStack()

    def din(name, shape, dt=F32):
        return nc.dram_tensor(name, list(shape), dt, kind="ExternalInput").ap()

    def dout(name, shape, dt=F32):
        return nc.dram_tensor(name, list(shape), dt, kind="ExternalOutput").ap()

    def dscr(name, shape, dt):
        return nc.dram_tensor(name, list(shape), dt).ap()

    xT = din("xT", [D, NTOK])
    cv = din("cv", [128, 16, 2])
    h0T = din("h0T", [DEPTH, 2, 128, 2048])
    w_mod = din("w_mod", [DEPTH, D, 9 * D])
    bmodT = din("bmodT", [DEPTH, 128, 144])
    normgT = din("normgT", [128, DEPTH, 6, 16])
    w1u = din("w_ffn1_up", [DEPTH, D, 2 * DFF])
    w1d = din("w_ffn1_down", [DEPTH, DFF, D])
    w_in = din("w_in", [DEPTH, D, INC])
    convwT = din("convwT", [128, DEPTH, 24, 5])
    convbT = din("convbT", [128, DEPTH, 24])
    alogT = din("alogT", [64, DEPTH])
    dtbT = din("dtbT", [64, DEPTH])
    dskB = din("dskB", [128, DEPTH, 2, 32])
    gssdT = din("gssdT", [128, DEPTH, 16])
    gsguT = din("gsguT", [128, DEPTH, 8])
    wspT = din("wspT", [128, DEPTH, 8, 128])
    bspB = din("bspB", [128, DEPTH, 8, 128])
    w_out = din("w_out", [DEPTH, 2 * D, D])
    w2u = din("w_ffn2_up", [DEPTH, D, 2 * DFF])
    w2d = din("w_ffn2_down", [DEPTH, DFF, D])
    cst = din("cst", [128, 6, 128])
    tabC = din("tabC", [128, 2, 512])
    tab256 = din("tab256", [128, 2, 512])
    tabL = din("tabL", [2, NSA, NSA])

    yT = dout("yT", [D, NTOK])
    nst = dout("nst", [DEPTH, 2, 2, 128, 2048])

    ZS = dscr("ZS", [D, NTOK], BF16)
    YC = dscr("YC", [2 * D, NTOK], BF16)
    XS = dscr("XS", [NCH, 128, 2048], BF16)
    BTK = dscr("BTK", [NCH, 128, 512], BF16)
    BT = dscr("BT", [128, 4, NTOK], BF16)
    CT = dscr("CT", [128, 4, NTOK], BF16)
    DTA = dscr("DTA", [NCH, 128, 128], F32)
    FCS = dscr("FCS", [NTOK, 2048], BF16)
    YTM = dscr("YTM", [NCH, 128, 2048], F32)

    TxT = T()
    TyT = Ts(NT)
    TZS = Ts(NT); TYC = Ts(NT)
    TXSd = Ts(NCH); TBTKd = Ts(NCH); TBTd = Ts(NT); TCTd = Ts(NT); TDTAd = Ts(NCH); TFCSd = Ts(NT); TYTMd = Ts(NCH)
    TYCs = Ts(NCH)
    TYCf = Ts(NT)
    Tnst = T()

    def sb(name, shape, dt):
        return es.enter_context(nc.sbuf_tensor(name, list(shape), dt))

    XT = sb("XT", [128, 16, 512], F32); TXT = Ts(16)
    H = sb("H", [128, 16, 512], BF16); TH = Ts(16)
    HID = sb("HID", [128, 44, 512], BF16); THID = Ts(44)
    FO = sb("FO", [128, 16, 512], F32); TFO = Ts(16)
    WS = [sb("WS%d" % i, [128, 11264], BF16) for i in range(2)]
    TWS = [Ts(2) for _ in range(2)]
    CST = sb("CST", [128, 6, 128], F32); TC = T()
    IDB = sb("IDB", [128, 128], BF16)
    ONESB = sb("ONESB", [128, 128], BF16)
    EPSB = sb("EPSB", [128, 1], F32)
    MOD = sb("MOD", [128, 144, 2], F32); TMOD = T()
    SA = [sb("SA%d" % i, [128, 16, 2], F32) for i in range(3)]
    GB = [sb("GB%d" % i, [128, 16, 2], F32) for i in range(3)]
    CV = sb("CV", [128, 16, 2], F32)
    SIL = sb("SIL", [128, 16, 2], BF16); TSIL = T()
    BMOD = sb("BMOD", [128, DEPTH, 144], F32)
    NG = sb("NG", [128, DEPTH, 6, 16], F32)
    CW = sb("CW", [128, DEPTH, 24, 5], F32)
    CBI = sb("CBI", [128, DEPTH, 24], F32)
    ALOG = sb("ALOG", [64, DEPTH], F32)
    ANEG = sb("ANEG", [64, DEPTH], F32)
    DTB = sb("DTB", [64, DEPTH], F32)
    DSK = sb("DSK", [128, DEPTH, 2, 32], F32)
    DSUM = sb("DSUM", [128, DEPTH, 32], F32)
    GSS = sb("GSS", [128, DEPTH, 16], F32)
    GSG = sb("GSG", [128, DEPTH, 8], F32)
    WSP = sb("WSP", [128, 8, 128], BF16); TWSP = T()
    BSP = sb("BSP", [128, 8, 128], F32)
    TABC = sb("TABC", [128, 2, 512], BF16)
    TAB256 = sb("TAB256", [128, 2, 512], BF16)
    RS = sb("RS", [128, 512], F32); TRS = T()
    TMP = [sb("TMP%d" % i, [128, 512], F32) for i in range(3)]; TTMP = Ts(3)
    SQ = [sb("SQ%d" % i, [128, 512], BF16) for i in range(2)]; TSQ = Ts(2)
    ST = [sb("ST%d" % i, [128, 512], BF16) for i in range(3)]; TST = Ts(3)
    DTT = TMP[0][0:64, :]; TDTT = TTMP[0]
    ATT = TMP[1][0:64, :]; TATT = TTMP[1]
    DTAst = TMP[2][:].rearrange("p (c f) -> p c f", c=4); TDTAst = TTMP[2]
    SM = sb("SM", [128, 8, 32], F32); TSM = Ts(8)
    VTK = [sb("VTK%d" % i, [128, 4, 128], BF16) for i in range(2)]; TVTK = Ts(2)

    P = [es.enter_context(nc.psum_tensor("P%d" % i, [128, 512], F32)) for i in range(8)]
    TP = Ts(8)
    PB = [P[6][:].bitcast(BF16)[:, 0:512], P[7][:].bitcast(BF16)[:, 0:512]]
    TPB = [TP[6], TP[7]]

    const_T = T()

    cl = [const_T]
    k.dma("sp", CST[:], cst, (), cl)
    k.dma("sp", CV[:], cv, (), cl)
    k.dma("sp", BMOD[:], bmodT.rearrange("l p c -> p l c"), (), cl)
    k.dma("sp", NG[:], normgT, (), cl)
    k.dma("sp", CW[:], convwT, (), cl)
    k.dma("sp", CBI[:], convbT, (), cl)
    k.dma("sp", ALOG[:], alogT, (), cl)
    k.dma("sp", DTB[:], dtbT, (), cl)
    k.dma("sp", DSK[:], dskB, (), cl)
    k.dma("sp", GSS[:], gssdT, (), cl)
    k.dma("sp", GSG[:], gsguT, (), cl)
    k.dma("pool", TABC[:], tabC, (), cl)
    k.dma("pool", TAB256[:], tab256, (), cl)
    k.vcopy(IDB[:], CST[:, 0, :], cl, cl)
    k.vcopy(ONESB[:], CST[:, 1, :], cl, cl)
    k.memset(EPSB[:], EPS, cl)
    k.act(ANEG[:], ALOG[:], AF.Exp, cl, cl)
    k.ts(ANEG[:], ANEG[:], -1.0, 0.0, ALU.mult, ALU.add, cl, cl)
    k.tt(DSUM[:], DSK[:, :, 0, :], DSK[:, :, 1, :], ALU.add, cl, cl)
    k.act(SIL[:], CV[:], AF.Silu, cl, [TSIL])
    IDF = CST[:, 0, :]; ONESF = CST[:, 1, :]
    TRI = [CST[:, 2, :], CST[:, 3, :]]
    STRICT = [CST[:, 4, :], CST[:, 5, :]]

    state = {"ws": 0, "bank": 0, "st": 0, "pb": 0, "vtk": 0}

    def next_st():
        i = state["st"]; state["st"] = (i + 1) % 3
        return ST[i], TST[i]

    def next_pb():
        i = state["pb"]; state["pb"] = (i + 1) % 2
        return PB[i], TPB[i]

    def gemm(Wd, KC, blocks, rhs_fn, cb, out_fn=None):
        cid = 0
        for blk in blocks:
            i = state["ws"]; state["ws"] = 1 - i
            width = sum(n for _, n in blk)
            view = WS[i][:, :KC * width].rearrange("p (k c) -> p k c", k=KC)
            off = 0
            for j, (c0, n) in enumerate(blk):
                wl = [TWS[i][j]] if len(blk) == 2 else TWS[i]
                k.dma("pool", view[:, :, off:off + n], Wd[:, c0:c0 + n].rearrange("(k p) c -> p k c", p=128), (), wl)
                off += n
            c = 0
            while c < width:
                m = min(128, width - c)
                if out_fn is None:
                    b = state["bank"]; state["bank"] = (b + 1) % 4
                    ps, Tps = P[b][:m, :], TP[b]
                else:
                    ps, Tps = out_fn(cid)
                for kc in range(KC):
                    rhs, Trhs = rhs_fn(kc)
                    k.mm(ps, view[:, kc, c:c + m], rhs, kc == 0, kc == KC - 1, TWS[i] + [Trhs], [Tps])
                cb(cid, ps, Tps, m)
                cid += 1
                c += m

    def sumsq_rstd(get, Tl, n, dim):
        for i in range(n):
            j = i % 2
            k.act(SQ[j][:], get(i), AF.Square, [Tl[i]], [TSQ[j]])
            k.mm(P[4][:], ONESB[:], SQ[j][:], i == 0, i == n - 1, [TSQ[j], const_T], [TP[4]])
        k.act(RS[:], P[4][:], AF.Sqrt, [TP[4], const_T], [TRS], bias=EPSB[:, 0:1], scale=1.0 / dim)
        k.recip(RS[:], RS[:], [TRS], [TRS])

    def load_x(tile, src, Tsrc):
        c0 = tile * 512
        k.dma("sp", XT[:], src[:, c0:c0 + 512].rearrange("(k p) t -> p k t", p=128), [Tsrc], TXT)

    def prologue(tile, s, src, Tsrc):
        v = 0 if tile == 0 else 1
        load_x(tile, src, Tsrc)
        sumsq_rstd(lambda fc: XT[:, fc, :], TXT, 16, D)
        for fc in range(16):
            j = fc % 2
            k.stt(TMP[j][:], XT[:, fc, :], SA[s][:, fc, v:v + 1], RS[:], ALU.mult, ALU.mult, [TXT[fc], TRS, TMOD], [TTMP[j]])
            k.act(H[:, fc, :], TMP[j][:], AF.Identity, [TTMP[j], TMOD], [TH[fc]], bias=MOD[:, 48 * s + fc, v:v + 1])

    def epilogue(tile, s):
        v = 0 if tile == 0 else 1
        c0 = tile * 512
        sumsq_rstd(lambda fc: FO[:, fc, :], TFO, 16, D)
        for fc in range(16):
            j = fc % 2
            k.tt(TMP[j][:], FO[:, fc, :], RS[:], ALU.mult, [TFO[fc], TRS], [TTMP[j]])
            k.stt(XT[:, fc, :], TMP[j][:], GB[s][:, fc, v:v + 1], XT[:, fc, :], ALU.mult, ALU.add, [TTMP[j], TMOD, TXT[fc]], [TXT[fc]])
        k.dma("sp", yT[:, c0:c0 + 512].rearrange("(k p) t -> p k t", p=128), XT[:], TXT, [TyT[tile]])

    def mod_phase(l):
        def out_fn(cid):
            return P[5][:, cid * 2:cid * 2 + 2], TP[5]
        gemm(w_mod[l], 16, [[(cb * 512, 512)] for cb in range(36)], lambda kc: (SIL[:, kc, :], TSIL), lambda *a: None, out_fn)
        k.tt(MOD[:], P[5][:, 0:288].rearrange("p (c v) -> p c v", v=2), bc(BMOD[:, l, :], [128, 144, 2], 2), ALU.add, [TP[5], const_T], [TMOD])
        for s in range(3):
            coef = 1.0 if s == 1 else 0.5
            k.stt(SA[s][:], MOD[:, 48 * s + 16:48 * s + 32, :], 1.0, bc(NG[:, l, 2 * s, :], [128, 16, 2], 2), ALU.add, ALU.mult, [TMOD, const_T], [TMOD])
            k.stt(GB[s][:], MOD[:, 48 * s + 32:48 * s + 48, :], coef, bc(NG[:, l, 2 * s + 1, :], [128, 16, 2], 2), ALU.mult, ALU.mult, [TMOD, const_T], [TMOD])

    GT = [TMP[0], TMP[1]]; TGT = [TTMP[0], TTMP[1]]

    def ffn(l, Wu, Wd, s, tile, src, Tsrc):
        prologue(tile, s, src, Tsrc)

        def cb_up(cid, ps, Tps, m):
            b, j = divmod(cid, 4)
            if j < 2:
                k.act(GT[j][:], ps, AF.Silu, [Tps], [TGT[j]])
            else:
                jj = b * 2 + (j - 2)
                k.tt(HID[:, jj, :], GT[j - 2][:], ps, ALU.mult, [TGT[j - 2], Tps], [THID[jj]])
        gemm(Wu[l], 16, [[(b * 256, 256), (DFF + b * 256, 256)] for b in range(22)], lambda kc: (H[:, kc, :], TH[kc]), cb_up)

        def cb_dn(cid, ps, Tps, m):
            k.acopy(FO[:, cid, :], ps, [Tps], [TFO[cid]])
        gemm(Wd[l], 44, [[(b * 256, 256)] for b in range(8)], lambda kc: (HID[:, kc, :], THID[kc]), cb_dn)
        epilogue(tile, s)

    FT = HID[:, 0:8, :]; TFT = THID[0:8]
    UT = HID[:, 8:16, :]; TUT = THID[8:16]
    VT = HID[:, 16:24, :]; TVT = THID[16:24]
    flat = lambda ap: ap.rearrange("p a f -> p (a f)")
    XSst = flat(HID[:, 24:40, :]).rearrange("p (c f) -> p c f", c=4)
    TXSst = THID[24:40]
    BTKst = flat(HID[:, 40:44, :]).rearrange("p (c f) -> p c f", c=4)
    TBTKst = THID[40:44]
    FCst = flat(FO[:, 0:8, :]).bitcast(BF16).rearrange("p (c f) -> p c f", c=4)
    TFCst = TFO[0:8]

    def m1(l, tile):
        c0 = tile * 512
        prologue(tile, 1, yT, TyT[tile])
        R_, L_ = (2, 256) if tile == 0 else (8, 64)
        blocks = [[(i * 512, 512)] for i in range(4)] + [[(2048 + i * 512, 512)] for i in range(6)] + [[(5120, 64)]] \
            + [[(5184 + i * 512, 512)] for i in range(2)] + [[(6208 + i * 512, 512)] for i in range(4)]

        def cb(cid, ps, Tps, m):
            if cid < 16:
                st, Tst = next_st()
                k.act(st[:], ps, AF.Silu, [Tps], [Tst])
                k.dma("sp", ZS[cid * 128:(cid + 1) * 128, c0:c0 + 512], st[:], [Tst], [TZS[tile]])
            elif cid < 40:
                xc = cid - 16
                acc = TMP[2]; Tacc = TTMP[2]
                k.ts(acc[:], ps, CW[:, l, xc, 2:3], CBI[:, l, xc:xc + 1], ALU.mult, ALU.add, [Tps, const_T], [Tacc])
                accv = acc[:].rearrange("p (r q) -> p r q", r=R_)
                psv = ps.rearrange("p (r q) -> p r q", r=R_)
                for kk, d in ((0, -2), (1, -1), (3, 1), (4, 2)):
                    lo, hi = max(0, -d), L_ - max(0, d)
                    k.stt(accv[:, :, lo:hi], psv[:, :, lo + d:hi + d], CW[:, l, xc, kk:kk + 1], accv[:, :, lo:hi], ALU.mult, ALU.add,
                          [Tps, const_T, Tacc], [Tacc])
                st, Tst = next_st()
                k.act(st[:], acc[:], AF.Silu, [Tacc], [Tst])
                if xc < 16 or xc < 20:
                    pb, Tpb = next_pb()
                    for c in range(4):
                        k.tr(pb[:, c * 128:(c + 1) * 128], st[:, c * 128:(c + 1) * 128], IDB[:], [Tst, const_T], [Tpb])
                    if xc < 16:
                        k.vcopy(XSst[:, :, xc * 128:(xc + 1) * 128], pb.rearrange("p (c f) -> p c f", c=4), [Tpb], TXSst)
                        if xc == 15:
                            k.dma("sp", XS[tile * 4:(tile + 1) * 4].rearrange("c p f -> p c f"), XSst, TXSst, TXSd[tile * 4:(tile + 1) * 4])
                    else:
                        g = xc - 16
                        k.dma("sp", BT[:, g, c0:c0 + 512], st[:], [Tst], [TBTd[tile]])
                        k.vcopy(BTKst[:, :, g * 128:(g + 1) * 128], pb.rearrange("p (c f) -> p c f", c=4), [Tpb], TBTKst)
                        if g == 3:
                            k.dma("sp", BTK[tile * 4:(tile + 1) * 4].rearrange("c p f -> p c f"), BTKst, TBTKst, TBTKd[tile * 4:(tile + 1) * 4])
                else:
                    g = xc - 20
                    k.dma("sp", CT[:, g, c0:c0 + 512], st[:], [Tst], [TCTd[tile]])
            elif cid == 40:
                k.act(ATT, ps, AF.Exp, [Tps, const_T], [TATT], bias=DTB[:, l:l + 1])
                k.act(DTT, ATT, AF.Ln, [TATT], [TDTT], bias=1.0)
                k.ts(ATT, DTT, ANEG[:, l:l + 1], 0.0, ALU.mult, ALU.add, [TDTT, const_T], [TATT])
                for c in range(4):
                    k.tr(P[4][:, c * 128:c * 128 + 64], DTT[:, c * 128:(c + 1) * 128], IDF[0:64, 0:64], [TDTT, const_T], [TP[4]])
                    k.tr(P[4][:, c * 128 + 64:c * 128 + 128], ATT[:, c * 128:(c + 1) * 128], IDF[0:64, 0:64], [TATT, const_T], [TP[4]])
                k.vcopy(DTAst, P[4][:].rearrange("p (c f) -> p c f", c=4), [TP[4]], [TDTAst])
                k.dma("sp", DTA[tile * 4:(tile + 1) * 4].rearrange("c p f -> p c f"), DTAst, [TDTAst], TDTAd[tile * 4:(tile + 1) * 4])
            elif cid < 49:
                fi = cid - 41
                k.acopy(FT[:, fi, :], ps, [Tps], [TFT[fi]])
                if fi == 7:
                    for c in range(4):
                        for g in range(4):
                            for hf in range(2):
                                k.mm(P[5][:], FT[:, 2 * g + hf, c * 128:(c + 1) * 128], TABC[:, hf, :], hf == 0, hf == 1,
                                     [TFT[2 * g + hf], const_T], [TP[5]])
                            dst = FCst[:, c, :].rearrange("p (two g q) -> p two g q", two=2, g=4)[:, :, g, :]
                            k.vcopy(dst, P[5][:].rearrange("p (two q) -> p two q", two=2), [TP[5]], TFCst)
                    if tile == 0:
                        for sq in range(2):
                            for j in range(8):
                                n = 0
                                for part in range(2):
                                    for cl_ in range(2):
                                        k.mm(P[5][:, 0:256], FCst[:, 2 * sq + cl_, part * 1024 + j * 128:part * 1024 + (j + 1) * 128],
                                             TAB256[:, cl_, part * 256:(part + 1) * 256], n == 0, n == 3, TFCst + [const_T], [TP[5]])
                                        n += 1
                                st, Tst = next_st()
                                k.acopy(st[:, 0:256], P[5][:, 0:256], [TP[5]], [Tst])
                                k.dma("sp", YC[2048 + j * 128:2048 + (j + 1) * 128, sq * 256:(sq + 1) * 256], st[:, 0:256], [Tst], [TYCf[0]])
                    else:
                        k.dma("sp", FCS[c0:c0 + 512].rearrange("(c p) f -> p c f", p=128), FCst, TFCst, [TFCSd[tile]])
            else:
                si = cid - 49
                A, TA = TMP[0], TTMP[0]
                B, TB = TMP[1], TTMP[1]
                k.acopy(A[:], ps, [Tps], [TA])
                k.tt(B[:], A[:], A[:], ALU.mult, [TA], [TB])
                k.ts(B[:], B[:], 0.044715, 1.0, ALU.mult, ALU.add, [TB], [TB])
                k.tt(B[:], B[:], A[:], ALU.mult, [TB, TA], [TB])
                k.act(B[:], B[:], AF.Sigmoid, [TB], [TB], scale=1.5957691216057308)
                if si < 8:
                    k.tt(UT[:, si, :], A[:], B[:], ALU.mult, [TA, TB], [TUT[si]])
                else:
                    h = si - 8
                    k.tt(VT[:, h, :], A[:], B[:], ALU.mult, [TA, TB], [TVT[h]])
                    if h == 7:
                        sumsq_rstd(lambda i: VT[:, i, :], TVT, 8, 1024)
                        for hh in range(8):
                            k.stt(VT[:, hh, :], VT[:, hh, :], GSG[:, l, hh:hh + 1], RS[:], ALU.mult, ALU.mult, [TVT[hh], TRS, const_T], [TVT[hh]])
                            pb, Tpb = next_pb()
                            for c in range(4):
                                k.tr(pb[:, c * 128:(c + 1) * 128], VT[:, hh, c * 128:(c + 1) * 128], IDB[:], [TVT[hh], const_T], [Tpb])
                            vi = state["vtk"]; state["vtk"] = 1 - vi
                            k.vcopy(VTK[vi][:], pb.rearrange("p (c f) -> p c f", c=4), [Tpb], [TVTK[vi]])
                            for c in range(4):
                                k.mm(P[5][:, c * 128:(c + 1) * 128], VTK[vi][:, c, :], WSP[:, hh, :], True, True, [TVTK[vi], TWSP], [TP[5]])
                            k.tt(A[:].rearrange("p (c t) -> p c t", c=4), P[5][:].rearrange("p (c t) -> p c t", c=4),
                                 bc(BSP[:, hh, :], [128, 4, 128], 1), ALU.add, [TP[5], TWSP], [TA])
                            st, Tst = next_st()
                            k.tt(st[:], A[:], UT[:, hh, :], ALU.mult, [TA, TUT[hh]], [Tst])
                            k.dma("sp", YC[3072 + hh * 128:3072 + (hh + 1) * 128, c0:c0 + 512], st[:], [Tst], [TYC[tile]])
        gemm(w_in[l], 16, blocks, lambda kc: (H[:, kc, :], TH[kc]), cb)

    HS = flat(XT[:, 0:4, :]); THS = TXT[0:4]
    RBUF = flat(FO[:, 0:8, :]).rearrange("p (h t) -> p h t", h=32); TRB = TFO[0:8]
    CBM = FO[:, 8, :].rearrange("p (g t) -> p g t", g=4); TCBM = [TFO[8]]
    LT = [FO[:, 9 + i, 0:128] for i in range(3)]; TLT = [[TFO[9 + i]] for i in range(3)]
    TMPA = FO[:, 12, :]; TTA = [TFO[12]]
    TMPB = FO[:, 13, :]; TTB = [TFO[13]]
    TMPC = FO[:, 14, :]; TTC = [TFO[14]]
    HB = flat(H[:, 0:4, :]); THB = TH[0:4]
    XSb = [flat(H[:, 4:8, :]), flat(H[:, 8:12, :])]; TXSb = [TH[4:8], TH[8:12]]
    XD = flat(H[:, 12:16, :]); TXD = TH[12:16]
    XW = flat(HID[:, 0:4, :]); TXW = THID[0:4]
    BTKb = [HID[:, 4, :], HID[:, 5, :]]; TBTKb = [[THID[4]], [THID[5]]]
    BTb = [HID[:, 6, :].rearrange("p (g t) -> p g t", g=4), HID[:, 7, :].rearrange("p (g t) -> p g t", g=4)]; TBTb = [[THID[6]], [THID[7]]]
    CTb = [HID[:, 8, :].rearrange("p (g t) -> p g t", g=4), HID[:, 9, :].rearrange("p (g t) -> p g t", g=4)]; TCTb = [[THID[8]], [THID[9]]]
    MT = [HID[:, 10 + i, 0:128] for i in range(3)]; TMT = [[THID[10 + i]] for i in range(3)]
    YAb = flat(HID[:, 16:20, :]); TYAb = THID[16:20]
    YTst = flat(HID[:, 20:24, :]).rearrange("p (fc t) -> p fc t", fc=16); TYTst = THID[20:24]
    DTAb = [TMP[0][:, 0:128], TMP[1][:, 0:128]]; TDTAb = [[TTMP[0]], [TTMP[1]]]
    TD = Ts(8)
    SMT = [[t] for t in TSM]

    def ssd_seq(l, chunks, h0, final):
        RB2 = flat(FO[:, 0:8, :])
        LTf = [FO[:, 9 + i, :] for i in range(3)]
        MTf = [HID[:, 10 + i, :] for i in range(3)]
        YA2 = [flat(XT[:, 4:8, :]), flat(XT[:, 8:12, :])]; TYA2 = [TXT[4:8], TXT[8:12]]
        YF2 = [flat(XT[:, 8:12, :]), flat(XT[:, 12:16, :])]; TYF2 = [TXT[8:12], TXT[12:16]]
        h4 = lambda ap: ap.rearrange("p (j t) -> p j t", j=4)
        for d in range(2):
            order = chunks if d == 0 else chunks[::-1]
            if h0 is not None:
                k.dma("sp", HS, h0[l, d], [const_T], THS)
                k.acopy(HB, HS, THS, THB)
            else:
                k.memset(HS, 0.0, THS)
                k.memset(HB, 0.0, THB)

            def loads(ci):
                c = order[ci]; q = ci % 2; tile = c // 4
                k.dma("sp", XSb[q], XS[c], [TXSd[c]], TXSb[q])
                k.dma("sp", BTKb[q], BTK[c], [TBTKd[c]], TBTKb[q])
                k.dma("sp", BTb[q], BT[:, :, c * 128:(c + 1) * 128], [TBTd[tile]], TBTb[q])
                k.dma("sp", CTb[q], CT[:, :, c * 128:(c + 1) * 128], [TCTd[tile]], TCTb[q])
                k.dma("sp", DTAb[q], DTA[c], [TDTAd[c]], TDTAb[q])
                if d == 1:
                    k.dma("sp", YF2[q], YTM[c], [TYTMd[c]], TYF2[q])

            if SSD_PF:
                loads(0)
            for ci, c in enumerate(order):
                q = ci % 2
                if SSD_PF:
                    if ci + 1 < len(order):
                        loads(ci + 1)
                else:
                    loads(ci)
                if d == 0:
                    YA, TYA = YA2[q], TYA2[q]
                else:
                    YA, TYA = YA2[0], TYA2[0]
                    YF, TYF = YF2[q], TYF2[q]
                dt_d = DTAb[q][:, 32 * d:32 * d + 32]
                a_d = DTAb[q][:, 64 + 32 * d:96 + 32 * d]
                CS, ECS, DTE, CD, W1 = SM[:, 0, :], SM[:, 1, :], SM[:, 2, :], SM[:, 3, :], SM[:, 4, :]
                k.mm(P[5][:, 0:32], TRI[d], a_d, True, True, [const_T] + TDTAb[q], [TP[5]])
                k.mm(P[5][:, 32:64], ONESF, a_d, True, True, [const_T] + TDTAb[q], [TP[5]])
                k.acopy(CS, P[5][:, 0:32], [TP[5]], [TSM[0]])
                k.act(ECS, P[5][:, 0:32], AF.Exp, [TP[5]], [TSM[1]])
                k.act(CD, P[5][:, 32:64], AF.Exp, [TP[5]], [TSM[3]])
                k.tt(DTE, P[5][:, 32:64], CS, ALU.subtract, [TP[5], TSM[0]], [TSM[2]])
                k.act(DTE, DTE, AF.Exp, [TSM[2]], [TSM[2]])
                k.tt(W1, dt_d, DTE, ALU.mult, TDTAb[q] + [TSM[2]], [TSM[4]])
                xs3 = XSb[q].rearrange("p (h e) -> p h e", h=32)
                k.tt(RBUF, bc(TRI[d], [128, 32, 128], 1), bc(a_d, [128, 32, 128], 2), ALU.mult, [const_T] + TDTAb[q], TRB)
                for g in range(4):
                    k.mm(P[0][:, g * 128:(g + 1) * 128], BTb[q][:, g, :], CTb[q][:, g, :], True, True, TBTb[q] + TCTb[q], [TP[0]])
                k.tt(CBM, P[0][:].rearrange("p (g t) -> p g t", g=4), bc(TRI[d], [128, 4, 128], 1), ALU.mult, [TP[0], const_T], TCBM)
                k.tt(XD.rearrange("p (h e) -> p h e", h=32), xs3, bc(dt_d, [128, 32, 64], 2), ALU.mult, TXSb[q] + TDTAb[q], TXD)
                k.tt(XW.rearrange("p (h e) -> p h e", h=32), xs3, bc(W1, [128, 32, 64], 2), ALU.mult, TXSb[q] + [TSM[4]], TXW)
                hq = 0
                for g in range(4):
                    gs = slice(g * 512, (g + 1) * 512)
                    for r in (range(8) if not SSD_B4 else ()):
                        hh = 8 * g + r
                        dq = hq % 2; lq = hq % 3; hq += 1
                        Dps = P[1 + dq][:, 0:128]
                        k.mm(Dps, STRICT[d], RBUF[:, hh, :], True, True, [const_T] + TRB, [TP[1 + dq]])
                        k.act(LT[lq], Dps, AF.Exp, [TP[1 + dq]], TLT[lq])
                        k.tt(MT[lq], LT[lq], CBM[:, g, :], ALU.mult, TLT[lq] + TCBM, TMT[lq])
                        k.mm(P[3][:, r * 64:(r + 1) * 64], MT[lq], XD[:, hh * 64:(hh + 1) * 64], True, True, TMT[lq] + TXD, [TP[3]])
                    for half in (range(2) if SSD_B4 else ()):
                        hb = 8 * g + 4 * half
                        dq = hq % 2; lq = hq % 3; hq += 1
                        Dps = P[1 + dq][:]
                        if SSD_B4 in (1, 3):
                            k.mm(Dps, STRICT[d], RB2[:, hb * 128:(hb + 4) * 128], True, True, [const_T] + TRB, [TP[1 + dq]])
                        else:
                            for j in range(4):
                                k.mm(Dps[:, j * 128:(j + 1) * 128], STRICT[d], RBUF[:, hb + j, :], True, True, [const_T] + TRB, [TP[1 + dq]])
                        k.act(LTf[lq], Dps, AF.Exp, [TP[1 + dq]], TLT[lq])
                        if SSD_B4 == 3:
                            for j in range(4):
                                k.tt(MTf[lq][:, j * 128:(j + 1) * 128], LTf[lq][:, j * 128:(j + 1) * 128], CBM[:, g, :], ALU.mult, TLT[lq] + TCBM, TMT[lq])
                        else:
                            k.tt(h4(MTf[lq]), h4(LTf[lq]), bc(CBM[:, g, :], [128, 4, 128], 1), ALU.mult, TLT[lq] + TCBM, TMT[lq])
                        for j in range(4):
                            hh = hb + j; r = 4 * half + j
                            k.mm(P[3][:, r * 64:(r + 1) * 64], MTf[lq][:, j * 128:(j + 1) * 128], XD[:, hh * 64:(hh + 1) * 64], True, True,
                                 TMT[lq] + TXD, [TP[3]])
                    k.mm(P[4][:], CTb[q][:, g, :], HB[:, gs], True, True, TCTb[q] + THB, [TP[4]])
                    k.mm(P[5][:], BTKb[q][:, g * 128:(g + 1) * 128], XW[:, gs], True, True, TBTKb[q] + TXW, [TP[5]])
                    v3 = lambda ap: ap.rearrange("p (h e) -> p h e", h=8)
                    k.tt(v3(TMPA), v3(P[4][:]), bc(ECS[:, 8 * g:8 * g + 8], [128, 8, 64], 2), ALU.mult, [TP[4], TSM[1]], TTA)
                    k.tt(YA[:, gs], P[3][:], TMPA, ALU.add, [TP[3]] + TTA, TYA)
                    if d == 0:
                        k.tt(v3(TMPB), v3(XSb[q][:, gs]), bc(DSUM[:, l, 8 * g:8 * g + 8], [128, 8, 64], 2), ALU.mult, TXSb[q] + [const_T], TTB)
                        k.tt(YA[:, gs], YA[:, gs], TMPB, ALU.add, TYA + TTB, TYA)
                    else:
                        k.tt(YA[:, gs], YA[:, gs], YF[:, gs], ALU.add, TYA + TYF, TYA)
                    k.tt(v3(TMPC), v3(HS[:, gs]), bc(CD[:, 8 * g:8 * g + 8], [128, 8, 64], 2), ALU.mult, THS + [TSM[3]], TTC)
                    k.tt(HS[:, gs], TMPC, P[5][:], ALU.add, TTC + [TP[5]], THS)
                    k.acopy(HB[:, gs], HS[:, gs], THS, THB)
                if d == 0:
                    k.dma(SSD_STQ, YTM[c], YA, TYA, [TYTMd[c]])
                else:
                    k.acopy(YAb, YA, TYA, TYAb)
                    for f4 in range(4):
                        pb, Tpb = next_pb()
                        for j in range(4):
                            fc = f4 * 4 + j
                            k.tr(pb[:, j * 128:(j + 1) * 128], YAb[:, fc * 128:(fc + 1) * 128], IDB[:], TYAb + [const_T], [Tpb])
                        k.vcopy(YTst[:, f4 * 4:(f4 + 1) * 4, :], pb.rearrange("p (j t) -> p j t", j=4), [Tpb], TYTst)
                    k.dma(SSD_STQ, YC[0:2048, c * 128:(c + 1) * 128].rearrange("(fc p) t -> p fc t", p=128), YTst, TYTst, [TYCs[c]])
            if final is not None:
                k.dma(SSD_STQ, final[d], HS, THS, [Tnst])

    FCb = [H[:, 4 * i:4 * i + 4, :].rearrange("p a f -> p (a f)") for i in range(4)]; TFCb = [TH[0], TH[4], TH[8], TH[12]]

    def fnet_sample(l):
        for kb in range(NSA // 512):
            for lc in range(NSA // 128):
                if lc % 4 == 0:
                    i = state["ws"]; state["ws"] = 1 - i
                    slab = WS[i][:, 0:4096].rearrange("p (tb c q) -> p tb c q", tb=2, c=4)
                    for tb in range(2):
                        k.dma("pool", slab[:, tb], tabL[tb, lc * 128:(lc + 4) * 128, kb * 512:(kb + 1) * 512].rearrange("(c p) q -> p c q", p=128),
                              (), [TWS[i][tb]])
                    cur = (slab, TWS[i])
                slab, Tsl = cur
                fq = lc % 4
                row0 = NPR + lc * 128
                k.dma("sp", FCb[fq], FCS[row0:row0 + 128, :], [TFCSd[row0 // 512]], [TFCb[fq]])
                for j in range(8):
                    k.mm(P[j][:], FCb[fq][:, j * 128:(j + 1) * 128], slab[:, 0, lc % 4, :], lc == 0, False, [TFCb[fq]] + Tsl, [TP[j]])
                    k.mm(P[j][:], FCb[fq][:, 1024 + j * 128:1024 + (j + 1) * 128], slab[:, 1, lc % 4, :], False, lc == NSA // 128 - 1, [TFCb[fq]] + Tsl, [TP[j]])
            for j in range(8):
                st, Tst = next_st()
                k.acopy(st[:], P[j][:], [TP[j]], [Tst])
                k.dma("sp", YC[2048 + j * 128:2048 + (j + 1) * 128, NPR + kb * 512:NPR + (kb + 1) * 512], st[:], [Tst], [TYCf[kb + 1]])

    def m3(l, tile):
        c0 = tile * 512
        deps = [TYC[tile], TYCf[tile]] + TYCs[tile * 4:(tile + 1) * 4]
        k.dma("sp", HID[:, 0:32, :], YC[:, c0:c0 + 512].rearrange("(k p) t -> p k t", p=128), deps, THID[0:32])
        k.dma("sp", H[:], ZS[:, c0:c0 + 512].rearrange("(k p) t -> p k t", p=128), [TZS[tile]], TH)
        for fc in range(16):
            k.tt(FO[:, fc, :], HID[:, fc, :], H[:, fc, :], ALU.mult, [THID[fc], TH[fc]], [TFO[fc]])
        sumsq_rstd(lambda fc: FO[:, fc, :], TFO, 16, D)
        for fc in range(16):
            k.stt(HID[:, fc, :], FO[:, fc, :], GSS[:, l, fc:fc + 1], RS[:], ALU.mult, ALU.mult, [TFO[fc], TRS, const_T], [THID[fc]])

        def cb(cid, ps, Tps, m):
            k.acopy(FO[:, cid, :], ps, [Tps], [TFO[cid]])
        gemm(w_out[l], 32, [[(b * 256, 256)] for b in range(8)], lambda kc: (HID[:, kc, :], THID[kc]), cb)
        load_x(tile, yT, TyT[tile])
        epilogue(tile, 1)

    for l in range(DEPTH):
        k.dma("pool", WSP[:], wspT[:, l], (), [TWSP])
        k.dma("sp", BSP[:], bspB[:, l], (), [TWSP])
        if "mod" in phases:
            mod_phase(l)
        if "ffn1" in phases:
            for t in range(NT):
                if l == 0:
                    ffn(l, w1u, w1d, 0, t, xT, TxT)
                else:
                    ffn(l, w1u, w1d, 0, t, yT, TyT[t])
        if "m1" in phases:
            for t in range(NT):
                m1(l, t)
        if "ssd" in phases:
            ssd_seq(l, [0, 1], None, [nst[l, 0, 0], nst[l, 0, 1]])
            ssd_seq(l, [2, 3], None, [nst[l, 1, 0], nst[l, 1, 1]])
            ssd_seq(l, list(range(4, NCH)), h0T, None)
        if "fnet" in phases:
            fnet_sample(l)
        if "m3" in phases:
            for t in range(NT):
                m3(l, t)
        if "ffn2" in phases:
            for t in range(NT):
                ffn(l, w2u, w2d, 2, t, yT, TyT[t])
    if stats is not None:
        stats.update({e: len(k.st[e]) for e in ENGS})
        stats.update({e + '_need': sum(1 for o in k.st[e] if o.need and not o.dma) for e in ENGS})
    k.emit()
    es.close()
    return nc


_CACHE = {}


def _consts(NSA=NSA):
    i = np.arange(128)
    ident = (i[:, None] == i[None, :]).astype(np.float32)
    ones = np.ones((128, 128), np.float32)
    le = (i[:, None] <= i[None, :]).astype(np.float32)
    ge = (i[:, None] >= i[None, :]).astype(np.float32)
    gt = (i[:, None] > i[None, :]).astype(np.float32)
    lt = (i[:, None] < i[None, :]).astype(np.float32)
    cst = np.stack([ident, ones, le, ge, gt, lt], axis=1).astype(np.float32)
    c = np.arange(256)
    ang = 2 * np.pi * np.outer(c, c) / 256.0
    cosC = np.cos(ang) / 16.0; sinC = np.sin(ang) / 16.0
    tabC = np.concatenate([cosC, sinC], axis=1).reshape(2, 128, 512).transpose(1, 0, 2)
    tab256 = np.concatenate([cosC, -sinC], axis=1).reshape(2, 128, 512).transpose(1, 0, 2)
    n = np.arange(NSA)
    lk = (np.outer(n, n) % NSA).astype(np.float64)
    angL = 2 * np.pi * lk / NSA
    sc = 1.0 / np.sqrt(float(NSA))
    tabL = np.stack([np.cos(angL) * sc, -np.sin(angL) * sc]).astype(np.float32)
    return (np.ascontiguousarray(cst), np.ascontiguousarray(tabC.astype(np.float32)),
            np.ascontiguousarray(tab256.astype(np.float32)), tabL)


def kernel(x_prompt, x_sample, state_ssd, c, c_ctx, w_mod, b_mod, norm_g, w_ffn1_up, w_ffn1_down, w_in, conv_w, conv_b,
           a_log, dt_bias, d_skip, g_ssd, g_sgu, w_sp, b_sp, w_out, w_ffn2_up, w_ffn2_down):
    f = lambda a: np.ascontiguousarray(np.asarray(a, dtype=np.float32))
    x_prompt, x_sample, state_ssd, c, c_ctx = map(f, (x_prompt, x_sample, state_ssd, c, c_ctx))
    if "nc" not in _CACHE:
        _CACHE["nc"] = build()
        _CACHE["consts"] = _consts()
    nc = _CACHE["nc"]
    cst, tabC, tab256, tabL = _CACHE["consts"]
    fm = lambda v, nch: f(np.asarray(v).reshape(nch, 128).T)
    shared = {
        "w_mod": f(w_mod), "w_ffn1_up": f(w_ffn1_up), "w_ffn1_down": f(w_ffn1_down), "w_in": f(w_in), "w_out": f(w_out),
        "w_ffn2_up": f(w_ffn2_up), "w_ffn2_down": f(w_ffn2_down),
        "bmodT": f(np.stack([fm(b_mod[l], 144) for l in range(DEPTH)])),
        "normgT": f(np.stack([np.stack([fm(norm_g[l][j], 16) for j in range(6)], axis=1) for l in range(DEPTH)], axis=1)),
        "convwT": f(np.stack([np.stack([fm(conv_w[l][kk], 24) for kk in range(5)], axis=2) for l in range(DEPTH)], axis=1)),
        "convbT": f(np.stack([fm(conv_b[l], 24) for l in range(DEPTH)], axis=1)),
        "alogT": f(np.asarray(a_log).reshape(DEPTH, 64).T), "dtbT": f(np.asarray(dt_bias).reshape(DEPTH, 64).T),
        "dskB": f(np.broadcast_to(np.asarray(d_skip)[None], (128, DEPTH, 2, 32))),
        "gssdT": f(np.stack([fm(g_ssd[l], 16) for l in range(DEPTH)], axis=1)),
        "gsguT": f(np.stack([fm(g_sgu[l], 8) for l in range(DEPTH)], axis=1)),
        "wspT": f(np.asarray(w_sp).transpose(3, 0, 1, 2)),
        "bspB": f(np.broadcast_to(np.asarray(b_sp)[None], (128, DEPTH, 8, 128))),
        "cst": cst, "tabC": tabC, "tab256": tab256, "tabL": tabL,
    }
    in_maps = []
    for i in range(8):
        s = i % 2
        xt = np.concatenate([x_prompt[2 * i:2 * i + 2].reshape(NPR, D), x_sample[s]], axis=0)
        m = dict(shared)
        m["xT"] = f(xt.T)
        m["cv"] = f(np.stack([fm(c_ctx, 16), fm(c[s], 16)], axis=2))
        m["h0T"] = f(state_ssd[s].transpose(0, 1, 4, 2, 3).reshape(DEPTH, 2, 128, 2048))
        in_maps.append(m)
    res = run_bass_kernel_spmd(nc, in_maps, core_ids=list(range(8)))
    outs = res.results
    y_prompt = np.empty((16, 256, D), np.float32)
    y_sample = np.empty((2, NSA, D), np.float32)
    new_state = np.empty((16, DEPTH, 2, 32, 64, 128), np.float32)
    for i in range(8):
        yt = np.asarray(outs[i]["yT"]).T
        y_prompt[2 * i:2 * i + 2] = yt[:NPR].reshape(2, 256, D)
        if i < 2:
            y_sample[i] = yt[NPR:]
        ns = np.asarray(outs[i]["nst"])
        new_state[2 * i:2 * i + 2] = ns.reshape(DEPTH, 2, 2, 128, 32, 64).transpose(1, 0, 2, 4, 5, 3)
    return (y_prompt, y_sample, new_state)
```

```python
import contextlib
import numpy as np
import concourse.bass as bass
import concourse.mybir as mybir
from concourse.bass_utils import run_bass_kernel_spmd

F32 = mybir.dt.float32
BF16 = mybir.dt.bfloat16
AF = mybir.ActivationFunctionType
ALU = mybir.AluOpType

D = 2048
DFF = 5632
NPR = 512
NSA = 4096
NTOK = NPR + NSA
NT = NTOK // 512
NCH = NTOK // 128
INC = 8256
EPS = 1e-6
DEPTH = 2
EPOCH = 12000
import os
SSD_STQ = os.environ.get('SSD_STQ', 'pool')
SSD_PF = int(os.environ.get('SSD_PF', '1'))
SSD_B4 = int(os.environ.get('SSD_B4', '0'))


class T:
    __slots__ = ("w", "rs")

    def __init__(s):
        s.w = None
        s.rs = []


def Ts(n):
    return [T() for _ in range(n)]


class Op:
    __slots__ = ("eng", "fn", "deps", "need", "sv", "dma", "dsem", "dval")

    def __init__(s, eng, fn, dma):
        s.eng = eng; s.fn = fn; s.deps = []; s.need = False; s.sv = 0; s.dma = dma; s.dsem = 0; s.dval = 0


ENGS = ("pe", "act", "dve", "pool", "sp")
DMAQ = ("pool", "sp")


class K:
    def __init__(s, nc):
        s.nc = nc
        s.st = {e: [] for e in ENGS}

    def op(s, eng, fn, r=(), w=(), dma=False):
        o = Op(eng, fn, dma)
        seen = set()
        for t in r:
            d = t.w
            if d is not None and id(d) not in seen:
                seen.add(id(d)); o.deps.append(d)
        for t in w:
            d = t.w
            if d is not None and id(d) not in seen:
                seen.add(id(d)); o.deps.append(d)
            last = {}
            for d in t.rs:
                last[d.eng if not d.dma else id(d)] = d
            for d in last.values():
                if id(d) not in seen:
                    seen.add(id(d)); o.deps.append(d)
        if eng == "pe":
            o.deps = [d for d in o.deps if not (d.eng == "pe" and not d.dma)]
        for d in o.deps:
            d.need = True
        for t in r:
            t.rs.append(o)
        for t in w:
            t.w = o; t.rs = []
        s.st[eng].append(o)
        return o

    def mm(s, out, lhsT, rhs, start, stop, r, w):
        return s.op("pe", lambda e: e.matmul(out, lhsT, rhs, start=start, stop=stop), r, w)

    def tr(s, out, in_, ident, r, w):
        return s.op("pe", lambda e: e.transpose(out, in_, ident), r, w)

    def act(s, out, in_, func, r, w, bias=None, scale=None):
        kw = {}
        if bias is not None: kw["bias"] = bias
        if scale is not None: kw["scale"] = scale
        return s.op("act", lambda e: e.activation(out, in_, func, **kw), r, w)

    def acopy(s, out, in_, r, w):
        return s.op("act", lambda e: e.copy(out, in_), r, w)

    def tt(s, out, a, b, op, r, w):
        return s.op("dve", lambda e: e.tensor_tensor(out, a, b, op), r, w)

    def ts(s, out, a, s1, s2, op0, op1, r, w):
        return s.op("dve", lambda e: e.tensor_scalar(out, a, s1, s2, op0, op1), r, w)

    def stt(s, out, a, sc, b, op0, op1, r, w):
        return s.op("dve", lambda e: e.scalar_tensor_tensor(out, a, sc, b, op0, op1), r, w)

    def vcopy(s, out, in_, r, w):
        return s.op("dve", lambda e: e.tensor_copy(out, in_), r, w)

    def recip(s, out, in_, r, w):
        return s.op("dve", lambda e: e.reciprocal(out, in_), r, w)

    def memset(s, ap, val, w):
        return s.op("dve", lambda e: e.memset(ap, val), (), w)

    def dma(s, q, out, in_, r, w):
        return s.op(q, lambda e: e.dma_start(out=out, in_=in_), r, w, dma=True)

    def emit(s, nsem_dma=14):
        nc = s.nc
        with contextlib.ExitStack() as es:
            nep = {}
            for e in ENGS:
                c = 0
                for o in s.st[e]:
                    if (not o.dma) and o.need:
                        c += 1; o.sv = c
                nep[e] = c // EPOCH + 1
            esem = {e: [es.enter_context(nc.semaphore("s_%s%d" % (e, i))) for i in range(nep[e])] for e in ENGS}
            dsems = {e: [es.enter_context(nc.semaphore("d_%s%d" % (e, i))) for i in range(nsem_dma)] for e in DMAQ}
            tot = {e: [0] * nsem_dma for e in DMAQ}
            for e in DMAQ:
                i = 0
                for o in s.st[e]:
                    if not o.dma: continue
                    o.dsem = i % nsem_dma
                    tot[e][o.dsem] += 16
                    o.dval = tot[e][o.dsem]
                    i += 1
            block = es.enter_context(nc.Block())

            def run(e, eng):
                waited = {}
                for o in s.st[e]:
                    ws = []
                    if o.dma and o.dval > 16:
                        ws.append((("d", e, o.dsem), dsems[e][o.dsem], o.dval - 16))
                    for d in o.deps:
                        if d.dma:
                            ws.append((("d", d.eng, d.dsem), dsems[d.eng][d.dsem], d.dval))
                        else:
                            ep, v = divmod(d.sv - 1, EPOCH)
                            ws.append((("e", d.eng, ep), esem[d.eng][ep], v + 1))
                    for key, sem, val in ws:
                        if waited.get(key, 0) >= val: continue
                        waited[key] = val
                        eng.wait_ge(sem, val)
                    ins = o.fn(eng)
                    if o.dma:
                        ins.then_inc(dsems[e][o.dsem], 16)
                    elif o.need:
                        ins.then_inc(esem[e][(o.sv - 1) // EPOCH], 1)
                if e in DMAQ:
                    for i in range(nsem_dma):
                        if tot[e][i] > 0:
                            eng.wait_ge(dsems[e][i], tot[e][i])

            block.tensor(lambda t: run("pe", t))
            block.scalar(lambda t: run("act", t))
            block.vector(lambda t: run("dve", t))
            block.gpsimd(lambda t: run("pool", t))
            block.sync(lambda t: run("sp", t))


def bc(ap, shape, axis):
    return ap.unsqueeze(axis).to_broadcast(shape)


def build(DEPTH=DEPTH, NSA=NSA, stats=None, phases="mod,ffn1,m1,ssd,fnet,m3,ffn2"):
    phases = set(phases.split(","))
    NTOK = NPR + NSA; NT = NTOK // 512; NCH = NTOK // 128
    nc = bass.Bass("TRN2", target_bir_lowering=False)
    k = K(nc)
    es = contextlib.ExitStack()

    def din(name, shape, dt=F32):
        return nc.dram_tensor(name, list(shape), dt, kind="ExternalInput").ap()

    def dout(name, shape, dt=F32):
        return nc.dram_tensor(name, list(shape), dt, kind="ExternalOutput").ap()

    def dscr(name, shape, dt):
        return nc.dram_tensor(name, list(shape), dt).ap()

    xT = din("xT", [D, NTOK])
    cv = din("cv", [128, 16, 2])
    h0T = din("h0T", [DEPTH, 2, 128, 2048])
    w_mod = din("w_mod", [DEPTH, D, 9 * D])
    bmodT = din("bmodT", [DEPTH, 128, 144])
    normgT = din("normgT", [128, DEPTH, 6, 16])
    w1u = din("w_ffn1_up", [DEPTH, D, 2 * DFF])
    w1d = din("w_ffn1_down", [DEPTH, DFF, D])
    w_in = din("w_in", [DEPTH, D, INC])
    convwT = din("convwT", [128, DEPTH, 24, 5])
    convbT = din("convbT", [128, DEPTH, 24])
    alogT = din("alogT", [64, DEPTH])
    dtbT = din("dtbT", [64, DEPTH])
    dskB = din("dskB", [128, DEPTH, 2, 32])
    gssdT = din("gssdT", [128, DEPTH, 16])
    gsguT = din("gsguT", [128, DEPTH, 8])
    wspT = din("wspT", [128, DEPTH, 8, 128])
    bspB = din("bspB", [128, DEPTH, 8, 128])
    w_out = din("w_out", [DEPTH, 2 * D, D])
    w2u = din("w_ffn2_up", [DEPTH, D, 2 * DFF])
    w2d = din("w_ffn2_down", [DEPTH, DFF, D])
    cst = din("cst", [128, 6, 128])
    tabC = din("tabC", [128, 2, 512])
    tab256 = din("tab256", [128, 2, 512])
    tabL = din("tabL", [2, NSA, NSA])

    yT = dout("yT", [D, NTOK])
    nst = dout("nst", [DEPTH, 2, 2, 128, 2048])

    ZS = dscr("ZS", [D, NTOK], BF16)
    YC = dscr("YC", [2 * D, NTOK], BF16)
    XS = dscr("XS", [NCH, 128, 2048], BF16)
    BTK = dscr("BTK", [NCH, 128, 512], BF16)
    BT = dscr("BT", [128, 4, NTOK], BF16)
    CT = dscr("CT", [128, 4, NTOK], BF16)
    DTA = dscr("DTA", [NCH, 128, 128], F32)
    FCS = dscr("FCS", [NTOK, 2048], BF16)
    YTM = dscr("YTM", [NCH, 128, 2048], F32)

    TxT = T()
    TyT = Ts(NT)
    TZS = Ts(NT); TYC = Ts(NT)
    TXSd = Ts(NCH); TBTKd = Ts(NCH); TBTd = Ts(NT); TCTd = Ts(NT); TDTAd = Ts(NCH); TFCSd = Ts(NT); TYTMd = Ts(NCH)
    TYCs = Ts(NCH)
    TYCf = Ts(NT)
    Tnst = T()

    def sb(name, shape, dt):
        return es.enter_context(nc.sbuf_tensor(name, list(shape), dt))

    XT = sb("XT", [128, 16, 512], F32); TXT = Ts(16)
    H = sb("H", [128, 16, 512], BF16); TH = Ts(16)
    HID = sb("HID", [128, 44, 512], BF16); THID = Ts(44)
    FO = sb("FO", [128, 16, 512], F32); TFO = Ts(16)
    WS = [sb("WS%d" % i, [128, 11264], BF16) for i in range(2)]
    TWS = [Ts(2) for _ in range(2)]
    CST = sb("CST", [128, 6, 128], F32); TC = T()
    IDB = sb("IDB", [128, 128], BF16)
    ONESB = sb("ONESB", [128, 128], BF16)
    EPSB = sb("EPSB", [128, 1], F32)
    MOD = sb("MOD", [128, 144, 2], F32); TMOD = T()
    SA = [sb("SA%d" % i, [128, 16, 2], F32) for i in range(3)]
    GB = [sb("GB%d" % i, [128, 16, 2], F32) for i in range(3)]
    CV = sb("CV", [128, 16, 2], F32)
    SIL = sb("SIL", [128, 16, 2], BF16); TSIL = T()
    BMOD = sb("BMOD", [128, DEPTH, 144], F32)
    NG = sb("NG", [128, DEPTH, 6, 16], F32)
    CW = sb("CW", [128, DEPTH, 24, 5], F32)
    CBI = sb("CBI", [128, DEPTH, 24], F32)
    ALOG = sb("ALOG", [64, DEPTH], F32)
    ANEG = sb("ANEG", [64, DEPTH], F32)
    DTB = sb("DTB", [64, DEPTH], F32)
    DSK = sb("DSK", [128, DEPTH, 2, 32], F32)
    DSUM = sb("DSUM", [128, DEPTH, 32], F32)
    GSS = sb("GSS", [128, DEPTH, 16], F32)
    GSG = sb("GSG", [128, DEPTH, 8], F32)
    WSP = sb("WSP", [128, 8, 128], BF16); TWSP = T()
    BSP = sb("BSP", [128, 8, 128], F32)
    TABC = sb("TABC", [128, 2, 512], BF16)
    TAB256 = sb("TAB256", [128, 2, 512], BF16)
    RS = sb("RS", [128, 512], F32); TRS = T()
    TMP = [sb("TMP%d" % i, [128, 512], F32) for i in range(3)]; TTMP = Ts(3)
    SQ = [sb("SQ%d" % i, [128, 512], BF16) for i in range(2)]; TSQ = Ts(2)
    ST = [sb("ST%d" % i, [128, 512], BF16) for i in range(3)]; TST = Ts(3)
    DTT = TMP[0][0:64, :]; TDTT = TTMP[0]
    ATT = TMP[1][0:64, :]; TATT = TTMP[1]
    DTAst = TMP[2][:].rearrange("p (c f) -> p c f", c=4); TDTAst = TTMP[2]
    SM = sb("SM", [128, 8, 32], F32); TSM = Ts(8)
    VTK = [sb("VTK%d" % i, [128, 4, 128], BF16) for i in range(2)]; TVTK = Ts(2)

    P = [es.enter_context(nc.psum_tensor("P%d" % i, [128, 512], F32)) for i in range(8)]
    TP = Ts(8)
    PB = [P[6][:].bitcast(BF16)[:, 0:512], P[7][:].bitcast(BF16)[:, 0:512]]
    TPB = [TP[6], TP[7]]

    const_T = T()

    cl = [const_T]
    k.dma("sp", CST[:], cst, (), cl)
    k.dma("sp", CV[:], cv, (), cl)
    k.dma("sp", BMOD[:], bmodT.rearrange("l p c -> p l c"), (), cl)
    k.dma("sp", NG[:], normgT, (), cl)
    k.dma("sp", CW[:], convwT, (), cl)
    k.dma("sp", CBI[:], convbT, (), cl)
    k.dma("sp", ALOG[:], alogT, (), cl)
    k.dma("sp", DTB[:], dtbT, (), cl)
    k.dma("sp", DSK[:], dskB, (), cl)
    k.dma("sp", GSS[:], gssdT, (), cl)
    k.dma("sp", GSG[:], gsguT, (), cl)
    k.dma("pool", TABC[:], tabC, (), cl)
    k.dma("pool", TAB256[:], tab256, (), cl)
    k.vcopy(IDB[:], CST[:, 0, :], cl, cl)
    k.vcopy(ONESB[:], CST[:, 1, :], cl, cl)
    k.memset(EPSB[:], EPS, cl)
    k.act(ANEG[:], ALOG[:], AF.Exp, cl, cl)
    k.ts(ANEG[:], ANEG[:], -1.0, 0.0, ALU.mult, ALU.add, cl, cl)
    k.tt(DSUM[:], DSK[:, :, 0, :], DSK[:, :, 1, :], ALU.add, cl, cl)
    k.act(SIL[:], CV[:], AF.Silu, cl, [TSIL])
    IDF = CST[:, 0, :]; ONESF = CST[:, 1, :]
    TRI = [CST[:, 2, :], CST[:, 3, :]]
    STRICT = [CST[:, 4, :], CST[:, 5, :]]

    state = {"ws": 0, "bank": 0, "st": 0, "pb": 0, "vtk": 0}

    def next_st():
        i = state["st"]; state["st"] = (i + 1) % 3
        return ST[i], TST[i]

    def next_pb():
        i = state["pb"]; state["pb"] = (i + 1) % 2
        return PB[i], TPB[i]

    def gemm(Wd, KC, blocks, rhs_fn, cb, out_fn=None):
        cid = 0
        for blk in blocks:
            i = state["ws"]; state["ws"] = 1 - i
            width = sum(n for _, n in blk)
            view = WS[i][:, :KC * width].rearrange("p (k c) -> p k c", k=KC)
            off = 0
            for j, (c0, n) in enumerate(blk):
                wl = [TWS[i][j]] if len(blk) == 2 else TWS[i]
                k.dma("pool", view[:, :, off:off + n], Wd[:, c0:c0 + n].rearrange("(k p) c -> p k c", p=128), (), wl)
                off += n
            c = 0
            while c < width:
                m = min(128, width - c)
                if out_fn is None:
                    b = state["bank"]; state["bank"] = (b + 1) % 4
                    ps, Tps = P[b][:m, :], TP[b]
                else:
                    ps, Tps = out_fn(cid)
                for kc in range(KC):
                    rhs, Trhs = rhs_fn(kc)
                    k.mm(ps, view[:, kc, c:c + m], rhs, kc == 0, kc == KC - 1, TWS[i] + [Trhs], [Tps])
                cb(cid, ps, Tps, m)
                cid += 1
                c += m

    def sumsq_rstd(get, Tl, n, dim):
        for i in range(n):
            j = i % 2
            k.act(SQ[j][:], get(i), AF.Square, [Tl[i]], [TSQ[j]])
            k.mm(P[4][:], ONESB[:], SQ[j][:], i == 0, i == n - 1, [TSQ[j], const_T], [TP[4]])
        k.act(RS[:], P[4][:], AF.Sqrt, [TP[4], const_T], [TRS], bias=EPSB[:, 0:1], scale=1.0 / dim)
        k.recip(RS[:], RS[:], [TRS], [TRS])

    def load_x(tile, src, Tsrc):
        c0 = tile * 512
        k.dma("sp", XT[:], src[:, c0:c0 + 512].rearrange("(k p) t -> p k t", p=128), [Tsrc], TXT)

    def prologue(tile, s, src, Tsrc):
        v = 0 if tile == 0 else 1
        load_x(tile, src, Tsrc)
        sumsq_rstd(lambda fc: XT[:, fc, :], TXT, 16, D)
        for fc in range(16):
            j = fc % 2
            k.stt(TMP[j][:], XT[:, fc, :], SA[s][:, fc, v:v + 1], RS[:], ALU.mult, ALU.mult, [TXT[fc], TRS, TMOD], [TTMP[j]])
            k.act(H[:, fc, :], TMP[j][:], AF.Identity, [TTMP[j], TMOD], [TH[fc]], bias=MOD[:, 48 * s + fc, v:v + 1])

    def epilogue(tile, s):
        v = 0 if tile == 0 else 1
        c0 = tile * 512
        sumsq_rstd(lambda fc: FO[:, fc, :], TFO, 16, D)
        for fc in range(16):
            j = fc % 2
            k.tt(TMP[j][:], FO[:, fc, :], RS[:], ALU.mult, [TFO[fc], TRS], [TTMP[j]])
            k.stt(XT[:, fc, :], TMP[j][:], GB[s][:, fc, v:v + 1], XT[:, fc, :], ALU.mult, ALU.add, [TTMP[j], TMOD, TXT[fc]], [TXT[fc]])
        k.dma("sp", yT[:, c0:c0 + 512].rearrange("(k p) t -> p k t", p=128), XT[:], TXT, [TyT[tile]])

    def mod_phase(l):
        def out_fn(cid):
            return P[5][:, cid * 2:cid * 2 + 2], TP[5]
        gemm(w_mod[l], 16, [[(cb * 512, 512)] for cb in range(36)], lambda kc: (SIL[:, kc, :], TSIL), lambda *a: None, out_fn)
        k.tt(MOD[:], P[5][:, 0:288].rearrange("p (c v) -> p c v", v=2), bc(BMOD[:, l, :], [128, 144, 2], 2), ALU.add, [TP[5], const_T], [TMOD])
        for s in range(3):
            coef = 1.0 if s == 1 else 0.5
            k.stt(SA[s][:], MOD[:, 48 * s + 16:48 * s + 32, :], 1.0, bc(NG[:, l, 2 * s, :], [128, 16, 2], 2), ALU.add, ALU.mult, [TMOD, const_T], [TMOD])
            k.stt(GB[s][:], MOD[:, 48 * s + 32:48 * s + 48, :], coef, bc(NG[:, l, 2 * s + 1, :], [128, 16, 2], 2), ALU.mult, ALU.mult, [TMOD, const_T], [TMOD])

    GT = [TMP[0], TMP[1]]; TGT = [TTMP[0], TTMP[1]]

    def ffn(l, Wu, Wd, s, tile, src, Tsrc):
        prologue(tile, s, src, Tsrc)

        def cb_up(cid, ps, Tps, m):
            b, j = divmod(cid, 4)
            if j < 2:
                k.act(GT[j][:], ps, AF.Silu, [Tps], [TGT[j]])
            else:
                jj = b * 2 + (j - 2)
                k.tt(HID[:, jj, :], GT[j - 2][:], ps, ALU.mult, [TGT[j - 2], Tps], [THID[jj]])
        gemm(Wu[l], 16, [[(b * 256, 256), (DFF + b * 256, 256)] for b in range(22)], lambda kc: (H[:, kc, :], TH[kc]), cb_up)

        def cb_dn(cid, ps, Tps, m):
            k.acopy(FO[:, cid, :], ps, [Tps], [TFO[cid]])
        gemm(Wd[l], 44, [[(b * 256, 256)] for b in range(8)], lambda kc: (HID[:, kc, :], THID[kc]), cb_dn)
        epilogue(tile, s)

    FT = HID[:, 0:8, :]; TFT = THID[0:8]
    UT = HID[:, 8:16, :]; TUT = THID[8:16]
    VT = HID[:, 16:24, :]; TVT = THID[16:24]
    flat = lambda ap: ap.rearrange("p a f -> p (a f)")
    XSst = flat(HID[:, 24:40, :]).rearrange("p (c f) -> p c f", c=4)
    TXSst = THID[24:40]
    BTKst = flat(HID[:, 40:44, :]).rearrange("p (c f) -> p c f", c=4)
    TBTKst = THID[40:44]
    FCst = flat(FO[:, 0:8, :]).bitcast(BF16).rearrange("p (c f) -> p c f", c=4)
    TFCst = TFO[0:8]

    def m1(l, tile):
        c0 = tile * 512
        prologue(tile, 1, yT, TyT[tile])
        R_, L_ = (2, 256) if tile == 0 else (8, 64)
        blocks = [[(i * 512, 512)] for i in range(4)] + [[(2048 + i * 512, 512)] for i in range(6)] + [[(5120, 64)]] \
            + [[(5184 + i * 512, 512)] for i in range(2)] + [[(6208 + i * 512, 512)] for i in range(4)]

        def cb(cid, ps, Tps, m):
            if cid < 16:
                st, Tst = next_st()
                k.act(st[:], ps, AF.Silu, [Tps], [Tst])
                k.dma("sp", ZS[cid * 128:(cid + 1) * 128, c0:c0 + 512], st[:], [Tst], [TZS[tile]])
            elif cid < 40:
                xc = cid - 16
                acc = TMP[2]; Tacc = TTMP[2]
                k.ts(acc[:], ps, CW[:, l, xc, 2:3], CBI[:, l, xc:xc + 1], ALU.mult, ALU.add, [Tps, const_T], [Tacc])
                accv = acc[:].rearrange("p (r q) -> p r q", r=R_)
                psv = ps.rearrange("p (r q) -> p r q", r=R_)
                for kk, d in ((0, -2), (1, -1), (3, 1), (4, 2)):
                    lo, hi = max(0, -d), L_ - max(0, d)
                    k.stt(accv[:, :, lo:hi], psv[:, :, lo + d:hi + d], CW[:, l, xc, kk:kk + 1], accv[:, :, lo:hi], ALU.mult, ALU.add,
                          [Tps, const_T, Tacc], [Tacc])
                st, Tst = next_st()
                k.act(st[:], acc[:], AF.Silu, [Tacc], [Tst])
                if xc < 16 or xc < 20:
                    pb, Tpb = next_pb()
                    for c in range(4):
                        k.tr(pb[:, c * 128:(c + 1) * 128], st[:, c * 128:(c + 1) * 128], IDB[:], [Tst, const_T], [Tpb])
                    if xc < 16:
                        k.vcopy(XSst[:, :, xc * 128:(xc + 1) * 128], pb.rearrange("p (c f) -> p c f", c=4), [Tpb], TXSst)
                        if xc == 15:
                            k.dma("sp", XS[tile * 4:(tile + 1) * 4].rearrange("c p f -> p c f"), XSst, TXSst, TXSd[tile * 4:(tile + 1) * 4])
                    else:
                        g = xc - 16
                        k.dma("sp", BT[:, g, c0:c0 + 512], st[:], [Tst], [TBTd[tile]])
                        k.vcopy(BTKst[:, :, g * 128:(g + 1) * 128], pb.rearrange("p (c f) -> p c f", c=4), [Tpb], TBTKst)
                        if g == 3:
                            k.dma("sp", BTK[tile * 4:(tile + 1) * 4].rearrange("c p f -> p c f"), BTKst, TBTKst, TBTKd[tile * 4:(tile + 1) * 4])
                else:
                    g = xc - 20
                    k.dma("sp", CT[:, g, c0:c0 + 512], st[:], [Tst], [TCTd[tile]])
            elif cid == 40:
                k.act(ATT, ps, AF.Exp, [Tps, const_T], [TATT], bias=DTB[:, l:l + 1])
                k.act(DTT, ATT, AF.Ln, [TATT], [TDTT], bias=1.0)
                k.ts(ATT, DTT, ANEG[:, l:l + 1], 0.0, ALU.mult, ALU.add, [TDTT, const_T], [TATT])
                for c in range(4):
                    k.tr(P[4][:, c * 128:c * 128 + 64], DTT[:, c * 128:(c + 1) * 128], IDF[0:64, 0:64], [TDTT, const_T], [TP[4]])
                    k.tr(P[4][:, c * 128 + 64:c * 128 + 128], ATT[:, c * 128:(c + 1) * 128], IDF[0:64, 0:64], [TATT, const_T], [TP[4]])
                k.vcopy(DTAst, P[4][:].rearrange("p (c f) -> p c f", c=4), [TP[4]], [TDTAst])
                k.dma("sp", DTA[tile * 4:(tile + 1) * 4].rearrange("c p f -> p c f"), DTAst, [TDTAst], TDTAd[tile * 4:(tile + 1) * 4])
            elif cid < 49:
                fi = cid - 41
                k.acopy(FT[:, fi, :], ps, [Tps], [TFT[fi]])
                if fi == 7:
                    for c in range(4):
                        for g in range(4):
                            bk = 5 - ((c * 4 + g) % 2)
                            for hf in range(2):
                                k.mm(P[bk][:], FT[:, 2 * g + hf, c * 128:(c + 1) * 128], TABC[:, hf, :], hf == 0, hf == 1,
                                     [TFT[2 * g + hf], const_T], [TP[bk]])
                            dst = FCst[:, c, :].rearrange("p (two g q) -> p two g q", two=2, g=4)[:, :, g, :]
                            k.vcopy(dst, P[bk][:].rearrange("p (two q) -> p two q", two=2), [TP[bk]], TFCst)
                    if tile == 0:
                        for sq in range(2):
                            for j in range(8):
                                n = 0
                                bk = 5 - (j % 2)
                                for part in range(2):
                                    for cl_ in range(2):
                                        k.mm(P[bk][:, 0:256], FCst[:, 2 * sq + cl_, part * 1024 + j * 128:part * 1024 + (j + 1) * 128],
                                             TAB256[:, cl_, part * 256:(part + 1) * 256], n == 0, n == 3, TFCst + [const_T], [TP[bk]])
                                        n += 1
                                st, Tst = next_st()
                                k.acopy(st[:, 0:256], P[bk][:, 0:256], [TP[bk]], [Tst])
                                k.dma("sp", YC[2048 + j * 128:2048 + (j + 1) * 128, sq * 256:(sq + 1) * 256], st[:, 0:256], [Tst], [TYCf[0]])
                    else:
                        k.dma("sp", FCS[c0:c0 + 512].rearrange("(c p) f -> p c f", p=128), FCst, TFCst, [TFCSd[tile]])
            else:
                si = cid - 49
                A, TA = TMP[0], TTMP[0]
                B, TB = TMP[1], TTMP[1]
                k.acopy(A[:], ps, [Tps], [TA])
                k.tt(B[:], A[:], A[:], ALU.mult, [TA], [TB])
                k.ts(B[:], B[:], 0.044715, 1.0, ALU.mult, ALU.add, [TB], [TB])
                k.tt(B[:], B[:], A[:], ALU.mult, [TB, TA], [TB])
                k.act(B[:], B[:], AF.Sigmoid, [TB], [TB], scale=1.5957691216057308)
                if si < 8:
                    k.tt(UT[:, si, :], A[:], B[:], ALU.mult, [TA, TB], [TUT[si]])
                else:
                    h = si - 8
                    k.tt(VT[:, h, :], A[:], B[:], ALU.mult, [TA, TB], [TVT[h]])
                    if h == 7:
                        sumsq_rstd(lambda i: VT[:, i, :], TVT, 8, 1024)
                        for hh in range(8):
                            k.stt(VT[:, hh, :], VT[:, hh, :], GSG[:, l, hh:hh + 1], RS[:], ALU.mult, ALU.mult, [TVT[hh], TRS, const_T], [TVT[hh]])
                            pb, Tpb = next_pb()
                            for c in range(4):
                                k.tr(pb[:, c * 128:(c + 1) * 128], VT[:, hh, c * 128:(c + 1) * 128], IDB[:], [TVT[hh], const_T], [Tpb])
                            vi = state["vtk"]; state["vtk"] = 1 - vi
                            k.vcopy(VTK[vi][:], pb.rearrange("p (c f) -> p c f", c=4), [Tpb], [TVTK[vi]])
                            bk = 5 - (hh % 2)
                            for c in range(4):
                                k.mm(P[bk][:, c * 128:(c + 1) * 128], VTK[vi][:, c, :], WSP[:, hh, :], True, True, [TVTK[vi], TWSP], [TP[bk]])
                            k.tt(A[:].rearrange("p (c t) -> p c t", c=4), P[bk][:].rearrange("p (c t) -> p c t", c=4),
                                 bc(BSP[:, hh, :], [128, 4, 128], 1), ALU.add, [TP[bk], TWSP], [TA])
                            st, Tst = next_st()
                            k.tt(st[:], A[:], UT[:, hh, :], ALU.mult, [TA, TUT[hh]], [Tst])
                            k.dma("sp", YC[3072 + hh * 128:3072 + (hh + 1) * 128, c0:c0 + 512], st[:], [Tst], [TYC[tile]])
        gemm(w_in[l], 16, blocks, lambda kc: (H[:, kc, :], TH[kc]), cb)

    HS = flat(XT[:, 0:4, :]); THS = TXT[0:4]
    RBUF = flat(FO[:, 0:8, :]).rearrange("p (h t) -> p h t", h=32); TRB = TFO[0:8]
    CBM = FO[:, 8, :].rearrange("p (g t) -> p g t", g=4); TCBM = [TFO[8]]
    LT = [FO[:, 9 + i, 0:128] for i in range(3)]; TLT = [[TFO[9 + i]] for i in range(3)]
    TMPA = FO[:, 12, :]; TTA = [TFO[12]]
    TMPB = FO[:, 13, :]; TTB = [TFO[13]]
    TMPC = FO[:, 14, :]; TTC = [TFO[14]]
    HB = flat(H[:, 0:4, :]); THB = TH[0:4]
    XSb = [flat(H[:, 4:8, :]), flat(H[:, 8:12, :])]; TXSb = [TH[4:8], TH[8:12]]
    XD = flat(H[:, 12:16, :]); TXD = TH[12:16]
    XW = flat(HID[:, 0:4, :]); TXW = THID[0:4]
    BTKb = [HID[:, 4, :], HID[:, 5, :]]; TBTKb = [[THID[4]], [THID[5]]]
    BTb = [HID[:, 6, :].rearrange("p (g t) -> p g t", g=4), HID[:, 7, :].rearrange("p (g t) -> p g t", g=4)]; TBTb = [[THID[6]], [THID[7]]]
    CTb = [HID[:, 8, :].rearrange("p (g t) -> p g t", g=4), HID[:, 9, :].rearrange("p (g t) -> p g t", g=4)]; TCTb = [[THID[8]], [THID[9]]]
    MT = [HID[:, 10 + i, 0:128] for i in range(3)]; TMT = [[THID[10 + i]] for i in range(3)]
    YAb = flat(HID[:, 16:20, :]); TYAb = THID[16:20]
    YTst = flat(HID[:, 20:24, :]).rearrange("p (fc t) -> p fc t", fc=16); TYTst = THID[20:24]
    DTAb = [TMP[0][:, 0:128], TMP[1][:, 0:128]]; TDTAb = [[TTMP[0]], [TTMP[1]]]
    TD = Ts(8)
    SMT = [[t] for t in TSM]

    def ssd_seq(l, chunks, h0, final):
        RB2 = flat(FO[:, 0:8, :])
        LTf = [FO[:, 9 + i, :] for i in range(3)]
        MTf = [HID[:, 10 + i, :] for i in range(3)]
        YA2 = [flat(XT[:, 4:8, :]), flat(XT[:, 8:12, :])]; TYA2 = [TXT[4:8], TXT[8:12]]
        YF2 = [flat(XT[:, 8:12, :]), flat(XT[:, 12:16, :])]; TYF2 = [TXT[8:12], TXT[12:16]]
        h4 = lambda ap: ap.rearrange("p (j t) -> p j t", j=4)
        for d in range(2):
            order = chunks if d == 0 else chunks[::-1]
            if h0 is not None:
                k.dma("sp", HS, h0[l, d], [const_T], THS)
                k.acopy(HB, HS, THS, THB)
            else:
                k.memset(HS, 0.0, THS)
                k.memset(HB, 0.0, THB)

            def loads(ci):
                c = order[ci]; q = ci % 2; tile = c // 4
                k.dma("sp", XSb[q], XS[c], [TXSd[c]], TXSb[q])
                k.dma("sp", BTKb[q], BTK[c], [TBTKd[c]], TBTKb[q])
                k.dma("sp", BTb[q], BT[:, :, c * 128:(c + 1) * 128], [TBTd[tile]], TBTb[q])
                k.dma("sp", CTb[q], CT[:, :, c * 128:(c + 1) * 128], [TCTd[tile]], TCTb[q])
                k.dma("sp", DTAb[q], DTA[c], [TDTAd[c]], TDTAb[q])
                if d == 1:
                    k.dma("sp", YF2[q], YTM[c], [TYTMd[c]], TYF2[q])

            if SSD_PF:
                loads(0)
            for ci, c in enumerate(order):
                q = ci % 2
                if SSD_PF:
                    if ci + 1 < len(order):
                        loads(ci + 1)
                else:
                    loads(ci)
                if d == 0:
                    YA, TYA = YA2[q], TYA2[q]
                else:
                    YA, TYA = YA2[0], TYA2[0]
                    YF, TYF = YF2[q], TYF2[q]
                dt_d = DTAb[q][:, 32 * d:32 * d + 32]
                a_d = DTAb[q][:, 64 + 32 * d:96 + 32 * d]
                CS, ECS, DTE, CD, W1 = SM[:, 0, :], SM[:, 1, :], SM[:, 2, :], SM[:, 3, :], SM[:, 4, :]
                k.mm(P[5][:, 0:32], TRI[d], a_d, True, True, [const_T] + TDTAb[q], [TP[5]])
                k.mm(P[5][:, 32:64], ONESF, a_d, True, True, [const_T] + TDTAb[q], [TP[5]])
                k.acopy(CS, P[5][:, 0:32], [TP[5]], [TSM[0]])
                k.act(ECS, P[5][:, 0:32], AF.Exp, [TP[5]], [TSM[1]])
                k.act(CD, P[5][:, 32:64], AF.Exp, [TP[5]], [TSM[3]])
                k.tt(DTE, P[5][:, 32:64], CS, ALU.subtract, [TP[5], TSM[0]], [TSM[2]])
                k.act(DTE, DTE, AF.Exp, [TSM[2]], [TSM[2]])
                k.tt(W1, dt_d, DTE, ALU.mult, TDTAb[q] + [TSM[2]], [TSM[4]])
                xs3 = XSb[q].rearrange("p (h e) -> p h e", h=32)
                k.tt(RBUF, bc(TRI[d], [128, 32, 128], 1), bc(a_d, [128, 32, 128], 2), ALU.mult, [const_T] + TDTAb[q], TRB)
                for g in range(4):
                    k.mm(P[0][:, g * 128:(g + 1) * 128], BTb[q][:, g, :], CTb[q][:, g, :], True, True, TBTb[q] + TCTb[q], [TP[0]])
                k.tt(CBM, P[0][:].rearrange("p (g t) -> p g t", g=4), bc(TRI[d], [128, 4, 128], 1), ALU.mult, [TP[0], const_T], TCBM)
                k.tt(XD.rearrange("p (h e) -> p h e", h=32), xs3, bc(dt_d, [128, 32, 64], 2), ALU.mult, TXSb[q] + TDTAb[q], TXD)
                k.tt(XW.rearrange("p (h e) -> p h e", h=32), xs3, bc(W1, [128, 32, 64], 2), ALU.mult, TXSb[q] + [TSM[4]], TXW)
                hq = 0
                for g in range(4):
                    gs = slice(g * 512, (g + 1) * 512)
                    for r in (range(8) if not SSD_B4 else ()):
                        hh = 8 * g + r
                        dq = hq % 2; lq = hq % 3; hq += 1
                        Dps = P[1 + dq][:, 0:128]
                        k.mm(Dps, STRICT[d], RBUF[:, hh, :], True, True, [const_T] + TRB, [TP[1 + dq]])
                        k.act(LT[lq], Dps, AF.Exp, [TP[1 + dq]], TLT[lq])
                        k.tt(MT[lq], LT[lq], CBM[:, g, :], ALU.mult, TLT[lq] + TCBM, TMT[lq])
                        k.mm(P[3][:, r * 64:(r + 1) * 64], MT[lq], XD[:, hh * 64:(hh + 1) * 64], True, True, TMT[lq] + TXD, [TP[3]])
                    for half in (range(2) if SSD_B4 else ()):
                        hb = 8 * g + 4 * half
                        dq = hq % 2; lq = hq % 3; hq += 1
                        Dps = P[1 + dq][:]
                        if SSD_B4 in (1, 3):
                            k.mm(Dps, STRICT[d], RB2[:, hb * 128:(hb + 4) * 128], True, True, [const_T] + TRB, [TP[1 + dq]])
                        else:
                            for j in range(4):
                                k.mm(Dps[:, j * 128:(j + 1) * 128], STRICT[d], RBUF[:, hb + j, :], True, True, [const_T] + TRB, [TP[1 + dq]])
                        k.act(LTf[lq], Dps, AF.Exp, [TP[1 + dq]], TLT[lq])
                        if SSD_B4 == 3:
                            for j in range(4):
                                k.tt(MTf[lq][:, j * 128:(j + 1) * 128], LTf[lq][:, j * 128:(j + 1) * 128], CBM[:, g, :], ALU.mult, TLT[lq] + TCBM, TMT[lq])
                        else:
                            k.tt(h4(MTf[lq]), h4(LTf[lq]), bc(CBM[:, g, :], [128, 4, 128], 1), ALU.mult, TLT[lq] + TCBM, TMT[lq])
                        for j in range(4):
                            hh = hb + j; r = 4 * half + j
                            k.mm(P[3][:, r * 64:(r + 1) * 64], MTf[lq][:, j * 128:(j + 1) * 128], XD[:, hh * 64:(hh + 1) * 64], True, True,
                                 TMT[lq] + TXD, [TP[3]])
                    k.mm(P[4][:], CTb[q][:, g, :], HB[:, gs], True, True, TCTb[q] + THB, [TP[4]])
                    k.mm(P[5][:], BTKb[q][:, g * 128:(g + 1) * 128], XW[:, gs], True, True, TBTKb[q] + TXW, [TP[5]])
                    v3 = lambda ap: ap.rearrange("p (h e) -> p h e", h=8)
                    k.tt(v3(TMPA), v3(P[4][:]), bc(ECS[:, 8 * g:8 * g + 8], [128, 8, 64], 2), ALU.mult, [TP[4], TSM[1]], TTA)
                    k.tt(YA[:, gs], P[3][:], TMPA, ALU.add, [TP[3]] + TTA, TYA)
                    if d == 0:
                        k.tt(v3(TMPB), v3(XSb[q][:, gs]), bc(DSUM[:, l, 8 * g:8 * g + 8], [128, 8, 64], 2), ALU.mult, TXSb[q] + [const_T], TTB)
                        k.tt(YA[:, gs], YA[:, gs], TMPB, ALU.add, TYA + TTB, TYA)
                    else:
                        k.tt(YA[:, gs], YA[:, gs], YF[:, gs], ALU.add, TYA + TYF, TYA)
                    k.tt(v3(TMPC), v3(HS[:, gs]), bc(CD[:, 8 * g:8 * g + 8], [128, 8, 64], 2), ALU.mult, THS + [TSM[3]], TTC)
                    k.tt(HS[:, gs], TMPC, P[5][:], ALU.add, TTC + [TP[5]], THS)
                    k.acopy(HB[:, gs], HS[:, gs], THS, THB)
                if d == 0:
                    k.dma(SSD_STQ, YTM[c], YA, TYA, [TYTMd[c]])
                else:
                    k.acopy(YAb, YA, TYA, TYAb)
                    for f4 in range(4):
                        pb, Tpb = next_pb()
                        for j in range(4):
                            fc = f4 * 4 + j
                            k.tr(pb[:, j * 128:(j + 1) * 128], YAb[:, fc * 128:(fc + 1) * 128], IDB[:], TYAb + [const_T], [Tpb])
                        k.vcopy(YTst[:, f4 * 4:(f4 + 1) * 4, :], pb.rearrange("p (j t) -> p j t", j=4), [Tpb], TYTst)
                    k.dma(SSD_STQ, YC[0:2048, c * 128:(c + 1) * 128].rearrange("(fc p) t -> p fc t", p=128), YTst, TYTst, [TYCs[c]])
            if final is not None:
                k.dma(SSD_STQ, final[d], HS, THS, [Tnst])

    FCb = [H[:, 4 * i:4 * i + 4, :].rearrange("p a f -> p (a f)") for i in range(4)]; TFCb = [TH[0], TH[4], TH[8], TH[12]]

    def fnet_sample(l):
        for kb in range(NSA // 512):
            for lc in range(NSA // 128):
                if lc % 4 == 0:
                    i = state["ws"]; state["ws"] = 1 - i
                    slab = WS[i][:, 0:4096].rearrange("p (tb c q) -> p tb c q", tb=2, c=4)
                    for tb in range(2):
                        k.dma("pool", slab[:, tb], tabL[tb, lc * 128:(lc + 4) * 128, kb * 512:(kb + 1) * 512].rearrange("(c p) q -> p c q", p=128),
                              (), [TWS[i][tb]])
                    cur = (slab, TWS[i])
                slab, Tsl = cur
                fq = lc % 4
                row0 = NPR + lc * 128
                k.dma("sp", FCb[fq], FCS[row0:row0 + 128, :], [TFCSd[row0 // 512]], [TFCb[fq]])
                for j in range(8):
                    k.mm(P[j][:], FCb[fq][:, j * 128:(j + 1) * 128], slab[:, 0, lc % 4, :], lc == 0, False, [TFCb[fq]] + Tsl, [TP[j]])
                    k.mm(P[j][:], FCb[fq][:, 1024 + j * 128:1024 + (j + 1) * 128], slab[:, 1, lc % 4, :], False, lc == NSA // 128 - 1, [TFCb[fq]] + Tsl, [TP[j]])
            for j in range(8):
                st, Tst = next_st()
                k.acopy(st[:], P[j][:], [TP[j]], [Tst])
                k.dma("sp", YC[2048 + j * 128:2048 + (j + 1) * 128, NPR + kb * 512:NPR + (kb + 1) * 512], st[:], [Tst], [TYCf[kb + 1]])

    def m3(l, tile):
        c0 = tile * 512
        deps = [TYC[tile], TYCf[tile]] + TYCs[tile * 4:(tile + 1) * 4]
        k.dma("sp", HID[:, 0:32, :], YC[:, c0:c0 + 512].rearrange("(k p) t -> p k t", p=128), deps, THID[0:32])
        k.dma("sp", H[:], ZS[:, c0:c0 + 512].rearrange("(k p) t -> p k t", p=128), [TZS[tile]], TH)
        for fc in range(16):
            k.tt(FO[:, fc, :], HID[:, fc, :], H[:, fc, :], ALU.mult, [THID[fc], TH[fc]], [TFO[fc]])
        sumsq_rstd(lambda fc: FO[:, fc, :], TFO, 16, D)
        for fc in range(16):
            k.stt(HID[:, fc, :], FO[:, fc, :], GSS[:, l, fc:fc + 1], RS[:], ALU.mult, ALU.mult, [TFO[fc], TRS, const_T], [THID[fc]])

        def cb(cid, ps, Tps, m):
            k.acopy(FO[:, cid, :], ps, [Tps], [TFO[cid]])
        gemm(w_out[l], 32, [[(b * 256, 256)] for b in range(8)], lambda kc: (HID[:, kc, :], THID[kc]), cb)
        load_x(tile, yT, TyT[tile])
        epilogue(tile, 1)

    for l in range(DEPTH):
        k.dma("pool", WSP[:], wspT[:, l], (), [TWSP])
        k.dma("sp", BSP[:], bspB[:, l], (), [TWSP])
        if "mod" in phases:
            mod_phase(l)
        if "ffn1" in phases:
            for t in range(NT):
                if l == 0:
                    ffn(l, w1u, w1d, 0, t, xT, TxT)
                else:
                    ffn(l, w1u, w1d, 0, t, yT, TyT[t])
        if "m1" in phases:
            for t in range(NT):
                m1(l, t)
        if "ssd" in phases:
            ssd_seq(l, [0, 1], None, [nst[l, 0, 0], nst[l, 0, 1]])
            ssd_seq(l, [2, 3], None, [nst[l, 1, 0], nst[l, 1, 1]])
            ssd_seq(l, list(range(4, NCH)), h0T, None)
        if "fnet" in phases:
            fnet_sample(l)
        if "m3" in phases:
            for t in range(NT):
                m3(l, t)
        if "ffn2" in phases:
            for t in range(NT):
                ffn(l, w2u, w2d, 2, t, yT, TyT[t])
    if stats is not None:
        stats.update({e: len(k.st[e]) for e in ENGS})
        stats.update({e + '_need': sum(1 for o in k.st[e] if o.need and not o.dma) for e in ENGS})
    k.emit()
    es.close()
    return nc


_CACHE = {}


def _consts(NSA=NSA):
    i = np.arange(128)
    ident = (i[:, None] == i[None, :]).astype(np.float32)
    ones = np.ones((128, 128), np.float32)
    le = (i[:, None] <= i[None, :]).astype(np.float32)
    ge = (i[:, None] >= i[None, :]).astype(np.float32)
    gt = (i[:, None] > i[None, :]).astype(np.float32)
    lt = (i[:, None] < i[None, :]).astype(np.float32)
    cst = np.stack([ident, ones, le, ge, gt, lt], axis=1).astype(np.float32)
    c = np.arange(256)
    ang = 2 * np.pi * np.outer(c, c) / 256.0
    cosC = np.cos(ang) / 16.0; sinC = np.sin(ang) / 16.0
    tabC = np.concatenate([cosC, sinC], axis=1).reshape(2, 128, 512).transpose(1, 0, 2)
    tab256 = np.concatenate([cosC, -sinC], axis=1).reshape(2, 128, 512).transpose(1, 0, 2)
    n = np.arange(NSA)
    lk = (np.outer(n, n) % NSA).astype(np.float64)
    angL = 2 * np.pi * lk / NSA
    sc = 1.0 / np.sqrt(float(NSA))
    tabL = np.stack([np.cos(angL) * sc, -np.sin(angL) * sc]).astype(np.float32)
    return (np.ascontiguousarray(cst), np.ascontiguousarray(tabC.astype(np.float32)),
            np.ascontiguousarray(tab256.astype(np.float32)), tabL)


def kernel(x_prompt, x_sample, state_ssd, c, c_ctx, w_mod, b_mod, norm_g, w_ffn1_up, w_ffn1_down, w_in, conv_w, conv_b,
           a_log, dt_bias, d_skip, g_ssd, g_sgu, w_sp, b_sp, w_out, w_ffn2_up, w_ffn2_down):
    f = lambda a: np.ascontiguousarray(np.asarray(a, dtype=np.float32))
    x_prompt, x_sample, state_ssd, c, c_ctx = map(f, (x_prompt, x_sample, state_ssd, c, c_ctx))
    if "nc" not in _CACHE:
        _CACHE["nc"] = build()
        _CACHE["consts"] = _consts()
    nc = _CACHE["nc"]
    cst, tabC, tab256, tabL = _CACHE["consts"]
    fm = lambda v, nch: f(np.asarray(v).reshape(nch, 128).T)
    shared = {
        "w_mod": f(w_mod), "w_ffn1_up": f(w_ffn1_up), "w_ffn1_down": f(w_ffn1_down), "w_in": f(w_in), "w_out": f(w_out),
        "w_ffn2_up": f(w_ffn2_up), "w_ffn2_down": f(w_ffn2_down),
        "bmodT": f(np.stack([fm(b_mod[l], 144) for l in range(DEPTH)])),
        "normgT": f(np.stack([np.stack([fm(norm_g[l][j], 16) for j in range(6)], axis=1) for l in range(DEPTH)], axis=1)),
        "convwT": f(np.stack([np.stack([fm(conv_w[l][kk], 24) for kk in range(5)], axis=2) for l in range(DEPTH)], axis=1)),
        "convbT": f(np.stack([fm(conv_b[l], 24) for l in range(DEPTH)], axis=1)),
        "alogT": f(np.asarray(a_log).reshape(DEPTH, 64).T), "dtbT": f(np.asarray(dt_bias).reshape(DEPTH, 64).T),
        "dskB": f(np.broadcast_to(np.asarray(d_skip)[None], (128, DEPTH, 2, 32))),
        "gssdT": f(np.stack([fm(g_ssd[l], 16) for l in range(DEPTH)], axis=1)),
        "gsguT": f(np.stack([fm(g_sgu[l], 8) for l in range(DEPTH)], axis=1)),
        "wspT": f(np.asarray(w_sp).transpose(3, 0, 1, 2)),
        "bspB": f(np.broadcast_to(np.asarray(b_sp)[None], (128, DEPTH, 8, 128))),
        "cst": cst, "tabC": tabC, "tab256": tab256, "tabL": tabL,
    }
    in_maps = []
    for i in range(8):
        s = i % 2
        xt = np.concatenate([x_prompt[2 * i:2 * i + 2].reshape(NPR, D), x_sample[s]], axis=0)
        m = dict(shared)
        m["xT"] = f(xt.T)
        m["cv"] = f(np.stack([fm(c_ctx, 16), fm(c[s], 16)], axis=2))
        m["h0T"] = f(state_ssd[s].transpose(0, 1, 4, 2, 3).reshape(DEPTH, 2, 128, 2048))
        in_maps.append(m)
    res = run_bass_kernel_spmd(nc, in_maps, core_ids=list(range(8)))
    outs = res.results
    y_prompt = np.empty((16, 256, D), np.float32)
    y_sample = np.empty((2, NSA, D), np.float32)
    new_state = np.empty((16, DEPTH, 2, 32, 64, 128), np.float32)
    for i in range(8):
        yt = np.asarray(outs[i]["yT"]).T
        y_prompt[2 * i:2 * i + 2] = yt[:NPR].reshape(2, 256, D)
        if i < 2:
            y_sample[i] = yt[NPR:]
        ns = np.asarray(outs[i]["nst"])
        new_state[2 * i:2 * i + 2] = ns.reshape(DEPTH, 2, 2, 128, 32, 64).transpose(1, 0, 2, 4, 5, 3)
    return (y_prompt, y_sample, new_state)
```
